# Optimizing a Trainium2 kernel written in Bass

```python
import jax, jax.numpy as jnp
from jax import lax
import numpy as np

D_MODEL = 1024
BATCH = 2
SEQ = 8192
DEPTH = 1

N_META = 16
BLOCK_Q = 128
META_PAD = BLOCK_Q - N_META
ATTN_HEADS = 8
HEAD_DIM = 64
ATTN_WIDTH = ATTN_HEADS * HEAD_DIM
CONV_GROUPS = 8
CONV_WIDTH = 512
CONV_K = 3
D_FF = 2816
NORM_EPS = 1e-6
IN_SPLITS = (ATTN_WIDTH, ATTN_WIDTH, ATTN_WIDTH, ATTN_HEADS,
             CONV_WIDTH, CONV_WIDTH, CONV_WIDTH, D_MODEL, D_MODEL)
IN_COLS = ATTN_WIDTH * 3 + ATTN_HEADS + CONV_WIDTH * 3 + D_MODEL * 2

kernel_name = "hybrid_fox_shortconv_macaron_layer"


def rms_norm(x, g):
    xf = x.astype(jnp.float32)
    y = xf * lax.rsqrt(jnp.mean(xf * xf, axis=-1, keepdims=True) + NORM_EPS)
    return (y * g.astype(jnp.float32)).astype(x.dtype)


def swiglu(x, w_in, w_out):
    a, b = jnp.split(x @ w_in, 2, axis=-1)
    return (jax.nn.silu(a) * b) @ w_out


def split_cols(z):
    idx, acc = [], 0
    for s in IN_SPLITS[:-1]:
        acc += s
        idx.append(acc)
    return jnp.split(z, idx, axis=-1)


def forgetting_attention(q, k, v, log_f):
    B, L, H, Dh = q.shape
    F = jnp.cumsum(log_f, axis=1)
    n_blocks = (L + META_PAD) // BLOCK_Q
    q_blocks = jnp.pad(q, ((0, 0), (META_PAD, 0), (0, 0), (0, 0)))
    q_blocks = q_blocks.reshape(B, n_blocks, BLOCK_Q, H, Dh).transpose(1, 0, 2, 3, 4)
    fq_blocks = jnp.pad(F, ((0, 0), (META_PAD, 0), (0, 0)))
    fq_blocks = fq_blocks.reshape(B, n_blocks, BLOCK_Q, H).transpose(1, 0, 3, 2)
    f_k = F.transpose(0, 2, 1)
    k_pos = jnp.arange(L)
    scale = HEAD_DIM ** -0.5

    def one_block(args):
        blk, qb, fqb = args
        q_pos = blk * BLOCK_Q + jnp.arange(BLOCK_Q) - META_PAD
        s = jnp.einsum('bqhd,bkhd->bhqk', qb, k).astype(jnp.float32) * scale
        s = s + fqb[..., None] - f_k[:, :, None, :]
        mask = k_pos[None, :] <= jnp.maximum(q_pos, 0)[:, None]
        s = jnp.where(mask[None, None], s, -jnp.inf)
        p = jax.nn.softmax(s, axis=-1).astype(v.dtype)
        return jnp.einsum('bhqk,bkhd->bqhd', p, v)

    out = lax.map(one_block, (jnp.arange(n_blocks), q_blocks, fq_blocks))
    out = out.transpose(1, 0, 2, 3, 4).reshape(B, n_blocks * BLOCK_Q, H, Dh)
    return out[:, META_PAD:]


def short_conv(u, w):
    L = u.shape[1]
    up = jnp.pad(u, ((0, 0), (CONV_K - 1, 0), (0, 0)))
    y = up[:, 0:L] * w[0]
    for j in range(1, CONV_K):
        y = y + up[:, j:j + L] * w[j]
    return y


def hybrid_layer(h, w_in, b_forget, conv_w, w_attn_branch, w_conv_branch, w_out,
                 g_ffn1_pre, g_ffn1_post, w_ffn1_in, w_ffn1_out,
                 g_mix_pre, g_mix_post, g_ffn2_pre, g_ffn2_post, w_ffn2_in, w_ffn2_out):
    B, L, _ = h.shape
    h = h + 0.5 * rms_norm(swiglu(rms_norm(h, g_ffn1_pre), w_ffn1_in, w_ffn1_out), g_ffn1_post)
    u = rms_norm(h, g_mix_pre)
    q, k, v, f_logit, c_b, c_c, c_in, gate_a, gate_c = split_cols(u @ w_in)
    q = q.reshape(B, L, ATTN_HEADS, HEAD_DIM)
    k = k.reshape(B, L, ATTN_HEADS, HEAD_DIM)
    v = v.reshape(B, L, ATTN_HEADS, HEAD_DIM)
    log_f = jax.nn.log_sigmoid((f_logit + b_forget).astype(jnp.float32))
    y_attn = forgetting_attention(q, k, v, log_f).reshape(B, L, ATTN_WIDTH) @ w_attn_branch
    y_conv = (c_b * short_conv(c_c * c_in, conv_w)) @ w_conv_branch
    mixed = (jax.nn.sigmoid(gate_a) * y_attn + jax.nn.sigmoid(gate_c) * y_conv) @ w_out
    h = h + rms_norm(mixed, g_mix_post)
    h = h + 0.5 * rms_norm(swiglu(rms_norm(h, g_ffn2_pre), w_ffn2_in, w_ffn2_out), g_ffn2_post)
    return h


def setup_inputs(seed: int = 0) -> dict:
    key = jax.random.key(seed)
    ks = jax.random.split(key, 24)
    nrm = lambda k, shape, scale: jax.random.normal(k, shape, jnp.float32) * scale
    gain = lambda k: 1.0 + 0.05 * jax.random.normal(k, (DEPTH, D_MODEL), jnp.float32)
    return {
        "x": nrm(ks[0], (BATCH, SEQ, D_MODEL), 1.0),
        "meta_tokens": nrm(ks[1], (N_META, D_MODEL), 1.0),
        "w_in": nrm(ks[2], (DEPTH, D_MODEL, IN_COLS), D_MODEL ** -0.5),
        "b_forget": nrm(ks[3], (DEPTH, ATTN_HEADS), 0.1),
        "conv_w": nrm(ks[4], (DEPTH, CONV_K, CONV_WIDTH), CONV_K ** -0.5),
        "w_attn_branch": nrm(ks[5], (DEPTH, ATTN_WIDTH, D_MODEL), ATTN_WIDTH ** -0.5),
        "w_conv_branch": nrm(ks[6], (DEPTH, CONV_WIDTH, D_MODEL), CONV_WIDTH ** -0.5),
        "w_out": nrm(ks[7], (DEPTH, D_MODEL, D_MODEL), D_MODEL ** -0.5),
        "g_ffn1_pre": gain(ks[8]),
        "g_ffn1_post": gain(ks[9]),
        "w_ffn1_in": nrm(ks[10], (DEPTH, D_MODEL, 2 * D_FF), D_MODEL ** -0.5),
        "w_ffn1_out": nrm(ks[11], (DEPTH, D_FF, D_MODEL), D_FF ** -0.5),
        "g_mix_pre": gain(ks[12]),
        "g_mix_post": gain(ks[13]),
        "g_ffn2_pre": gain(ks[14]),
        "g_ffn2_post": gain(ks[15]),
        "w_ffn2_in": nrm(ks[16], (DEPTH, D_MODEL, 2 * D_FF), D_MODEL ** -0.5),
        "w_ffn2_out": nrm(ks[17], (DEPTH, D_FF, D_MODEL), D_FF ** -0.5),
    }


def reference(x, meta_tokens, w_in, b_forget, conv_w, w_attn_branch, w_conv_branch, w_out,
              g_ffn1_pre, g_ffn1_post, w_ffn1_in, w_ffn1_out,
              g_mix_pre, g_mix_post, g_ffn2_pre, g_ffn2_post, w_ffn2_in, w_ffn2_out):
    B = x.shape[0]
    meta = jnp.broadcast_to(meta_tokens.astype(x.dtype)[None], (B, N_META, x.shape[-1]))
    h = jnp.concatenate([meta, x], axis=1)
    for l in range(DEPTH):
        h = hybrid_layer(h, w_in[l], b_forget[l], conv_w[l], w_attn_branch[l], w_conv_branch[l], w_out[l],
                         g_ffn1_pre[l], g_ffn1_post[l], w_ffn1_in[l], w_ffn1_out[l],
                         g_mix_pre[l], g_mix_post[l], g_ffn2_pre[l], g_ffn2_post[l],
                         w_ffn2_in[l], w_ffn2_out[l])
    return h[:, N_META:]
```

```python
import numpy as np
import concourse.bass as bass
import concourse.mybir as mybir
from concourse.bass_utils import run_bass_kernel_spmd

F32 = mybir.dt.float32
BF16 = mybir.dt.bfloat16
AF = mybir.ActivationFunctionType
ALU = mybir.AluOpType

NCORES = 8
D = 1024
KC = 8
SEQ = 8192
NMETA = 16
TL = 2048
EX = 32
NT = TL + EX
TW = 416
NTILE = NT // TW
DFF = 2816
FC = DFF // 128
INC = 5128
H = 8
DH = 64
EPS = 1e-6
NBLK = 65
NDMA_SEMS = 28


class Buf:
    __slots__ = ("name", "last_w", "reads")

    def __init__(self, name):
        self.name = name
        self.last_w = None
        self.reads = []


class KB:
    ENGS = ("pe", "act", "dve", "pool", "sp")

    def __init__(self, nc):
        self.nc = nc
        self.sem = {e: nc.alloc_semaphore("sem_" + e) for e in self.ENGS}
        self.cnt = {e: 0 for e in self.ENGS}
        self.waited = {e: {} for e in self.ENGS}
        self.prog = {e: [] for e in self.ENGS}
        self.dma_sems = [nc.alloc_semaphore("dsem%d" % i) for i in range(NDMA_SEMS)]
        self.dma_val = [0] * NDMA_SEMS
        self.dma_next = 0
        self.cc_sem = nc.alloc_semaphore("ccsem")
        self.cc_val = 0

    def _semh(self, key):
        if key[0] == "eng":
            return self.sem[key[1]]
        if key[0] == "dma":
            return self.dma_sems[key[1]]
        return self.cc_sem

    def _deps(self, engine, reads, writes, extra=()):
        deps = {}

        def add(tok):
            if tok is None:
                return
            k, v = tok
            if deps.get(k, 0) < v:
                deps[k] = v
        for r in reads:
            add(r.last_w)
        for w in writes:
            add(w.last_w)
            for t in w.reads:
                add(t)
        for t in extra:
            add(t)
        waits = []
        for k, v in deps.items():
            if k == ("eng", "pe") and engine == "pe":
                continue
            if self.waited[engine].get(k, 0) >= v:
                continue
            self.waited[engine][k] = v
            waits.append((self._semh(k), v))
        return waits

    def _commit(self, tok, reads, writes):
        for r in reads:
            r.reads.append(tok)
            if len(r.reads) > 64:
                best = {}
                for k, v in r.reads:
                    if best.get(k, 0) < v:
                        best[k] = v
                r.reads = list(best.items())
        for w in writes:
            w.last_w = tok
            w.reads = []

    def op(self, engine, fn, reads=(), writes=(), extra=(), inc=True):
        if isinstance(fn, MMop):
            inc = fn.stop
        waits = self._deps(engine, reads, writes, extra)
        if inc:
            self.cnt[engine] += 1
            tok = (("eng", engine), self.cnt[engine])
        else:
            tok = (("eng", engine), self.cnt[engine] + 1)
        sem = self.sem[engine]

        def thunk(eng, waits=waits, fn=fn, sem=sem, inc=inc):
            for s, v in waits:
                eng.wait_ge(s, v)
            ins = fn(eng)
            if inc:
                ins.then_inc(sem, 1)
        self.prog[engine].append(thunk)
        self._commit(tok, reads, writes)
        return tok

    def _next_dma_idx(self, engine):
        if engine == "pool":
            self.pool_next = (getattr(self, "pool_next", -1) + 1) % 4
            return NDMA_SEMS - 4 + self.pool_next
        idx = self.dma_next
        self.dma_next = (self.dma_next + 1) % (NDMA_SEMS - 4)
        return idx

    def dma(self, engine, out, in_, reads=(), writes=(), extra=()):
        idx = self._next_dma_idx(engine)
        prev = self.dma_val[idx]
        ex = list(extra)
        if prev:
            ex.append((("dma", idx), prev))
        waits = self._deps(engine, reads, writes, ex)
        self.dma_val[idx] = prev + 16
        tok = (("dma", idx), prev + 16)
        sem = self.dma_sems[idx]

        def thunk(eng, waits=waits, sem=sem, out=out, in_=in_):
            for s, v in waits:
                eng.wait_ge(s, v)
            eng.dma_start(out=out, in_=in_).then_inc(sem, 16)
        self.prog[engine].append(thunk)
        self._commit(tok, reads, writes)
        return tok

    def collective(self, ins, outs, reads=(), writes=()):
        waits = self._deps("pool", reads, writes)
        self.cc_val += 1
        tok = (("cc", 0), self.cc_val)
        sem = self.cc_sem
        groups = [list(range(NCORES))]

        def thunk(eng, waits=waits):
            for s, v in waits:
                eng.wait_ge(s, v)
            eng.collective_compute("AllGather", ALU.bypass, replica_groups=groups,
                                   ins=ins, outs=outs).then_inc(sem)
        self.prog["pool"].append(thunk)
        self._commit(tok, reads, writes)
        return tok

    def all_tokens(self):
        toks = [(("eng", e), self.cnt[e]) for e in self.ENGS if self.cnt[e]]
        toks += [(("dma", i), self.dma_val[i]) for i in range(NDMA_SEMS) if self.dma_val[i]]
        if self.cc_val:
            toks.append((("cc", 0), self.cc_val))
        return toks

    def barrier(self):
        toks = self.all_tokens()
        for e in self.ENGS:
            waits = self._deps(e, (), (), [t for t in toks if t[0] != ("eng", e)])

            def thunk(eng, waits=waits):
                for s, v in waits:
                    eng.wait_ge(s, v)
            self.prog[e].append(thunk)

    def emit(self):
        nc = self.nc
        with nc.Block() as block:
            @block.tensor
            def _(e):
                for t in self.prog["pe"]:
                    t(e)

            @block.scalar
            def _(e):
                for t in self.prog["act"]:
                    t(e)

            @block.vector
            def _(e):
                for t in self.prog["dve"]:
                    t(e)

            @block.gpsimd
            def _(e):
                for t in self.prog["pool"]:
                    t(e)

            @block.sync
            def _(e):
                for t in self.prog["sp"]:
                    t(e)


class MMop:
    def __init__(self, out, lhsT, rhs, start, stop, inc=None):
        self.a = (out, lhsT, rhs, start, stop)
        self.stop = stop if inc is None else inc

    def __call__(self, e):
        out, lhsT, rhs, start, stop = self.a
        return e.matmul(out, lhsT=lhsT, rhs=rhs, start=start, stop=stop, skip_group_check=True)


def MM(out, lhsT, rhs, start, stop, inc=None):
    return MMop(out, lhsT, rhs, start, stop, inc)


def ACTF(out, in_, func, bias=None, scale=1.0):
    if bias is None:
        return lambda e: e.activation(out=out, in_=in_, func=func, scale=scale)
    return lambda e: e.activation(out=out, in_=in_, func=func, bias=bias, scale=scale)


def CP(out, in_):
    return lambda e: e.tensor_copy(out=out, in_=in_)


def TT(out, in0, in1, op):
    return lambda e: e.tensor_tensor(out=out, in0=in0, in1=in1, op=op)


def TS(out, in0, s1, s2, op0, op1=None):
    if op1 is None:
        return lambda e: e.tensor_scalar(out=out, in0=in0, scalar1=s1, scalar2=None, op0=op0)
    return lambda e: e.tensor_scalar(out=out, in0=in0, scalar1=s1, scalar2=s2, op0=op0, op1=op1)


def STT(out, in0, scalar, in1, op0, op1):
    return lambda e: e.scalar_tensor_tensor(out=out, in0=in0, scalar=scalar, in1=in1, op0=op0, op1=op1)


def MEMSET(ap, val):
    return lambda e: e.memset(ap, val)


def RECIP(out, in_):
    return lambda e: e.reciprocal(out=out, in_=in_)


class Ring:
    def __init__(self, items):
        self.items = items
        self.i = 0

    def next(self):
        it = self.items[self.i % len(self.items)]
        self.i += 1
        return it


WSPEC = [
    ("f1in", D, 2 * DFF), ("f1out", DFF, D), ("win", D, INC), ("wab", 512, D), ("wcb", 512, D),
    ("wout", D, D), ("f2in", D, 2 * DFF), ("f2out", DFF, D),
]


def build_nc(stop=None):
    nc = bass.Bass("TRN2", target_bir_lowering=False)
    nc.allow_low_precision("bf16 matmul operands with fp32 PSUM accumulation")
    kb = KB(nc)

    x_in = nc.dram_tensor("x", [TL, D], F32, kind="ExternalInput").ap()
    xe_in = nc.dram_tensor("xe", [EX, D], F32, kind="ExternalInput").ap()
    gn_in = nc.dram_tensor("gn", [128, 48], F32, kind="ExternalInput").ap()
    cw_in = nc.dram_tensor("cw", [128, 12], F32, kind="ExternalInput").ap()
    bfg_in = nc.dram_tensor("bfg", [H, 1], F32, kind="ExternalInput").ap()
    cst_in = nc.dram_tensor("cst", [128, 640], F32, kind="ExternalInput").ap()
    y_out = nc.dram_tensor("y", [TL, D], F32, kind="ExternalOutput").ap()
    wsh, wsend, wfull, Bwsend, Bwfull = {}, {}, {}, {}, {}
    for name, K, N in WSPEC:
        wsh[name] = nc.dram_tensor("w_" + name, [K // NCORES, N], F32, kind="ExternalInput").ap()
        wsend[name] = nc.dram_tensor("ws_" + name, [K // NCORES, N], BF16).ap()
        wfull[name] = nc.dram_tensor("wf_" + name, [K, N], BF16).ap()
        Bwsend[name] = Buf("ws_" + name)
        Bwfull[name] = Buf("wf_" + name)
    S1 = [nc.dram_tensor("S1_%d" % k, [H, DH, NT], BF16).ap() for k in range(3)]
    G1 = [nc.dram_tensor("G1_%d" % k, [NCORES, H, DH, NT], BF16).ap() for k in range(3)]
    SF = nc.dram_tensor("SF", [H, NT], F32).ap()
    GF = nc.dram_tensor("GF", [NCORES, H, NT], F32).ap()
    S2 = nc.dram_tensor("S2", [NCORES, DH, TL], BF16).ap()
    G2 = nc.dram_tensor("G2", [H, NCORES, DH, TL], BF16).ap()
    BS1, BG1, BSF, BGF, BS2, BG2 = [Buf(n) for n in ("S1", "G1", "SF", "GF", "S2", "G2")]
    L1 = [nc.dram_tensor("L1_%d" % k, [NCORES, DH, NT], BF16).ap() for k in range(3)]
    LF = nc.dram_tensor("LF", [NCORES, NT], F32).ap()
    L2 = nc.dram_tensor("L2", [H, DH, TL], BF16).ap()
    BL1, BLF, BL2 = [Buf(n) for n in ("L1", "LF", "L2")]

    PS = [nc.alloc_psum_tensor("ps%d" % i, [128, 512], F32) for i in range(8)]
    BPS = [Buf("ps%d" % i) for i in range(8)]

    def sb(name, shape, dt):
        return nc.alloc_sbuf_tensor("sb_" + name, shape, dt)

    hT = sb("hT", [128, KC, NT], F32)
    BhT = [Buf("hT%d" % t) for t in range(NTILE)]
    AR = 28800
    arena = sb("arena", [128, AR], BF16)
    uT = sb("uT", [128, KC, TW], BF16)
    BuT = Buf("uT")
    yTf = sb("yT", [128, KC * TW], F32)
    Byt = [Buf("yt%d" % i) for i in range(3)]
    sa = [sb("sa%d" % i, [128, TW], F32) for i in range(2)]
    Bsa = [Buf("sa%d" % i) for i in range(2)]
    tmp = [sb("tmp%d" % i, [128, TW], F32) for i in range(2)]
    Btmp = [Buf("tmp%d" % i) for i in range(2)]
    rstd = sb("rstd", [128, TW], F32)
    Brstd = Buf("rstd")
    sq = [sb("sq%d" % i, [128, TW], BF16) for i in range(2)]
    Bsq = [Buf("sq%d" % i) for i in range(2)]
    stage = [sb("stage%d" % i, [128, TW], BF16) for i in range(3)]
    Bstage = [Buf("stage%d" % i) for i in range(3)]
    lfT = sb("lfT", [H, TW], F32)
    BlfT = Buf("lfT")
    wf8 = sb("wf8", [128, KC, 8], BF16)
    Bwf8 = Buf("wf8")
    gn = sb("gn", [128, 48], F32)
    gh = sb("gh", [128, 48], F32)
    cw = sb("cw", [128, 12], F32)
    bfg = sb("bfg", [H, 1], F32)
    nbf = sb("nbf", [H, 1], F32)
    cst = sb("cst", [128, 640], F32)
    cbf = sb("cbf", [128, 384], BF16)
    epst = sb("epst", [128, 1], F32)
    onet = sb("onet", [128, 1], F32)
    Bconst = Buf("const")
    ccbuf = sb("ccbuf", [128, 4, TW + 2], F32)
    Bcc = Buf("cc")
    cbT = sb("cbT", [128, 4, TW], F32)
    BcbT = Buf("cbT")
    convg = sb("convg", [128, 4, TW], BF16)
    Bconvg = Buf("convg")
    attnT = sb("attnT", [128, 4, TW], BF16)
    BattnT = Buf("attnT")
    sig = [sb("sig%d" % i, [128, TW], F32) for i in range(2)]
    Bsig = [Buf("sig%d" % i) for i in range(2)]
    PT = [sb("PT%d" % i, [128, 512], BF16) for i in range(4)]
    BPT = [Buf("PT%d" % i) for i in range(4)]
    osb = [sb("osb%d" % i, [128, 512], F32) for i in range(2)]
    Bosb = [Buf("osb%d" % i) for i in range(2)]
    den = [sb("den%d" % i, [64, 512], F32) for i in range(2)]
    Bden = [Buf("den%d" % i) for i in range(2)]
    abf = [sb("abf%d" % i, [64, 512], BF16) for i in range(2)]
    Babf = [Buf("abf%d" % i) for i in range(2)]
    lfR = sb("lfR", [NBLK, 128], F32)
    lfC = sb("lfC", [128, NBLK], F32)
    totT = sb("totT", [NBLK, 128], F32)
    negF = sb("negF", [128, NBLK], F32)
    Ff = sb("Ff", [128, NBLK], F32)
    Fp = sb("Fp", [128, NBLK, 2], BF16)
    BlfR, BlfC, BtotT, BnegF, BFf, BFp = [Buf(n) for n in ("lfR", "lfC", "totT", "negF", "Ff", "Fp")]

    def v3(off, k, c):
        return arena[:, off:off + k * c].rearrange("p (k c) -> p k c", k=k)

    gT = v3(0, FC, TW)
    BgT = Buf("gT")
    off = FC * TW
    wA = [v3(off + i * 2048, KC, 256) for i in range(4)]
    BwA = [Buf("wA%d" % i) for i in range(4)]
    off += 4 * 2048
    wO = [v3(off + i * 5632, FC, 256) for i in range(2)]
    BwO = [Buf("wO%d" % i) for i in range(2)]
    wBr = [v3(off + i * 5632, 4, 1024) for i in range(2)]
    off += 2 * 5632
    assert off <= AR
    Qaug = arena[:, 0:SEQ]
    Kaug = arena[:, SEQ:SEQ + SEQ + NMETA]
    o2 = 2 * SEQ + NMETA
    Vaug = v3(o2, NBLK, 128)
    o2 += NBLK * 128
    VTst = arena[:, o2:o2 + TL]
    assert o2 + TL <= AR
    BQaug, BKaug, BVaug, BVTst = [Buf(n) for n in ("Qaug", "Kaug", "Vaug", "VTst")]

    I32 = cst[:, 0:128]
    U32 = cst[:, 128:256]
    SU32 = cst[:, 256:384]
    ONE32 = cst[:, 384:512]
    Ibf = cbf[:, 0:128]
    TRIbf = cbf[:, 128:256]
    ONEbf = cbf[:, 256:384]

    def finish():
        toks = kb.all_tokens()
        waits = kb._deps("sp", (), (), toks)

        def fin(eng, waits=waits):
            for s_, v in waits:
                eng.wait_ge(s_, v)
        kb.prog["sp"].append(fin)
        kb.emit()
        return nc

    kb.dma("sp", cst[:, :], cst_in, writes=[Bconst])
    kb.dma("sp", gn[:, :], gn_in, writes=[Bconst])
    kb.dma("sp", cw[:, :], cw_in, writes=[Bconst])
    kb.dma("sp", bfg[:, :], bfg_in, writes=[Bconst])
    kb.op("dve", CP(cbf[:, 0:128], cst[:, 0:128]), reads=[Bconst], writes=[Bconst])
    kb.op("dve", CP(cbf[:, 128:256], cst[:, 512:640]), reads=[Bconst], writes=[Bconst])
    kb.op("dve", CP(cbf[:, 256:384], cst[:, 384:512]), reads=[Bconst], writes=[Bconst])
    kb.op("dve", TS(gh[:, :], gn[:, :], 0.5, None, ALU.mult), reads=[Bconst], writes=[Bconst])
    kb.op("dve", TS(nbf[:, :], bfg[:, :], -1.0, None, ALU.mult), reads=[Bconst], writes=[Bconst])
    kb.op("dve", MEMSET(epst[:, :], EPS), writes=[Bconst])
    kb.op("dve", MEMSET(onet[:, :], 1.0), writes=[Bconst])

    for name, K, N in WSPEC:
        kb.dma("pool", wsend[name], wsh[name], writes=[Bwsend[name]])
        kb.collective([wsend[name]], [wfull[name]], reads=[Bwsend[name]], writes=[Bwfull[name]])

    xs = [yTf[:, 0:1024], yTf[:, 1024:2048]]
    psr = Ring([4, 5, 6])
    cpe = Ring(["dve", "act"])
    nblk_in = 1 + TL // 128
    for bi in range(nblk_in):
        rows = EX if bi == 0 else 128
        c0 = 0 if bi == 0 else EX + (bi - 1) * 128
        src = xe_in if bi == 0 else x_in[(bi - 1) * 128: bi * 128, :]
        xb = xs[bi % 2]
        Bx = Byt[bi % 2]
        kb.dma("sp", xb[0:rows, :], src, writes=[Bx])
        tt = c0 // TW
        tts = sorted(set([c0 // TW, (c0 + rows - 1) // TW]))
        for half in range(2):
            pi = psr.next()
            for q in range(4):
                kc = half * 4 + q
                kb.op("pe", MM(PS[pi][:, q * 128: q * 128 + rows], xb[0:rows, kc * 128:(kc + 1) * 128],
                               I32[0:rows, 0:rows], True, True), reads=[Bx, Bconst], writes=[BPS[pi]])
            eng = cpe.next()
            outv = hT[:, half * 4:(half + 1) * 4, c0:c0 + rows]
            inv = PS[pi][:, :].rearrange("p (q c) -> p q c", q=4)[:, :, 0:rows]
            if eng == "dve":
                kb.op("dve", CP(outv, inv), reads=[BPS[pi]], writes=[BhT[t] for t in tts])
            else:
                kb.op("act", ACTF(outv, inv, AF.Copy), reads=[BPS[pi]], writes=[BhT[t] for t in tts])

    if stop == "1":
        return finish()
    main_ring = Ring([0, 1, 2, 3, 4, 5])
    aux_ring = Ring([6, 7])
    wA_ring = Ring([0, 1, 2, 3])
    wO_ring = Ring([0, 1])
    sa_ring = Ring([0, 1])
    tmp_ring = Ring([0, 1])
    sq_ring = Ring([0, 1])
    st_ring = Ring([0, 1, 2])
    sig_ring = Ring([0, 1])

    def wview(name):
        return wfull[name].rearrange("(kc p) n -> p kc n", p=128)

    def rms_stats(src_fn, Bsrc, nch):
        pi = aux_ring.next()
        for kc in range(nch):
            si = sq_ring.next()
            eng = PELT if kc % 2 == 0 else "dve"
            a = src_fn(kc)
            kb.op(eng, TT(sq[si][:, :], a, a, ALU.mult), reads=Bsrc, writes=[Bsq[si]])
            kb.op("pe", MM(PS[pi][:, 0:TW], ONEbf, sq[si][:, :], kc == 0, kc == nch - 1, inc=True),
                  reads=[Bsq[si], Bconst], writes=[BPS[pi]])
        kb.op("act", ACTF(rstd[:, :], PS[pi][:, 0:TW], AF.Ln, bias=epst[:, 0:1], scale=1.0 / D),
              reads=[BPS[pi], Bconst], writes=[Brstd])
        kb.op("act", ACTF(rstd[:, :], rstd[:, :], AF.Exp, scale=-0.5), reads=[Brstd], writes=[Brstd])

    def make_u(t, gidx):
        c0 = t * TW
        rms_stats(lambda kc: hT[:, kc, c0:c0 + TW], [BhT[t]], KC)
        for kc in range(KC):
            kb.op("dve", STT(uT[:, kc, :], hT[:, kc, c0:c0 + TW], gn[:, gidx * 8 + kc: gidx * 8 + kc + 1],
                             rstd[:, :], ALU.mult, ALU.mult),
                  reads=[BhT[t], Brstd, Bconst], writes=[BuT])

    def post_norm_add(t, gcol_fn):
        c0 = t * TW
        rms_stats(lambda kc: yTf[:, kc * TW:(kc + 1) * TW], Byt, KC)
        for kc in range(KC):
            ti = tmp_ring.next()
            kb.op("dve", STT(tmp[ti][:, :], yTf[:, kc * TW:(kc + 1) * TW], gcol_fn(kc), rstd[:, :],
                             ALU.mult, ALU.mult), reads=Byt + [Brstd, Bconst], writes=[Btmp[ti]])
            kb.op("pool", TT(hT[:, kc, c0:c0 + TW], hT[:, kc, c0:c0 + TW], tmp[ti][:, :], ALU.add),
                  reads=[Btmp[ti], BhT[t]], writes=[BhT[t]])

    class Pipe:
        def __init__(self, depth):
            self.items = []
            self.depth = depth

        def add(self, load, compute):
            self.items.append((load, compute))

        def run(self, limit=None):
            if limit is not None:
                self.items = self.items[:limit]
            n = len(self.items)
            for i in range(min(self.depth, n)):
                if self.items[i][0]:
                    self.items[i][0]()
            for i in range(n):
                self.items[i][1]()
                j = i + self.depth
                if j < n and self.items[j][0]:
                    self.items[j][0]()

    def ffn_items(pipe, t, win, wout, gpre, gpost):
        c0 = t * TW
        pipe.add(None, lambda: make_u(t, gpre))
        wv = wview(win)
        for g in range(FC // 2):
            bufs = {}

            def load(g=g, bufs=bufs):
                ia, ib = wA_ring.next(), wA_ring.next()
                bufs["a"], bufs["b"] = ia, ib
                kb.dma("sp", wA[ia], wv[:, :, g * 256:(g + 1) * 256], reads=[Bwfull[win]], writes=[BwA[ia]])
                kb.dma("sp", wA[ib], wv[:, :, DFF + g * 256: DFF + (g + 1) * 256], reads=[Bwfull[win]],
                       writes=[BwA[ib]])

            def compute(g=g, bufs=bufs):
                ia, ib = bufs["a"], bufs["b"]
                for jj in range(2):
                    j = g * 2 + jj
                    pa, pb = main_ring.next(), main_ring.next()
                    for kc in range(KC):
                        kb.op("pe", MM(PS[pa][:, 0:TW], wA[ia][:, kc, jj * 128:(jj + 1) * 128], uT[:, kc, :],
                                       kc == 0, kc == KC - 1), reads=[BwA[ia], BuT], writes=[BPS[pa]])
                    for kc in range(KC):
                        kb.op("pe", MM(PS[pb][:, 0:TW], wA[ib][:, kc, jj * 128:(jj + 1) * 128], uT[:, kc, :],
                                       kc == 0, kc == KC - 1), reads=[BwA[ib], BuT], writes=[BPS[pb]])
                    si = sa_ring.next()
                    kb.op("act", ACTF(sa[si][:, :], PS[pa][:, 0:TW], AF.Silu), reads=[BPS[pa]], writes=[Bsa[si]])
                    kb.op("dve", TT(gT[:, j, :], sa[si][:, :], PS[pb][:, 0:TW], ALU.mult),
                          reads=[Bsa[si], BPS[pb]], writes=[BgT])
            pipe.add(load, compute)
        wo = wfull[wout].rearrange("(fc p) n -> p fc n", p=128)
        for g in range(4):
            bufs = {}

            def load(g=g, bufs=bufs):
                io = wO_ring.next()
                bufs["o"] = io
                kb.dma("sp", wO[io], wo[:, :, g * 256:(g + 1) * 256], reads=[Bwfull[wout]], writes=[BwO[io]])

            def compute(g=g, bufs=bufs):
                io = bufs["o"]
                for jj in range(2):
                    dc = g * 2 + jj
                    pi = main_ring.next()
                    for fc in range(FC):
                        kb.op("pe", MM(PS[pi][:, 0:TW], wO[io][:, fc, jj * 128:(jj + 1) * 128], gT[:, fc, :],
                                       fc == 0, fc == FC - 1), reads=[BwO[io], BgT], writes=[BPS[pi]])
                    kb.op("act", ACTF(yTf[:, dc * TW:(dc + 1) * TW], PS[pi][:, 0:TW], AF.Copy),
                          reads=[BPS[pi]], writes=Byt)
            pipe.add(load, compute)
        pipe.add(None, lambda: post_norm_add(t, lambda kc: gh[:, gpost * 8 + kc: gpost * 8 + kc + 1]))

    def proj_items(pipe, wname, col0, nchunks, consumer):
        wv = wview(wname)
        ngrp = (nchunks + 1) // 2
        for g in range(ngrp):
            bufs = {}
            ncg = min(2, nchunks - g * 2)

            def load(g=g, bufs=bufs, ncg=ncg):
                ia = wA_ring.next()
                bufs["a"] = ia
                kb.dma("sp", wA[ia][:, :, 0:ncg * 128], wv[:, :, col0 + g * 256: col0 + g * 256 + ncg * 128],
                       reads=[Bwfull[wname]], writes=[BwA[ia]])

            def compute(g=g, bufs=bufs, ncg=ncg):
                ia = bufs["a"]
                for jj in range(ncg):
                    ci = g * 2 + jj
                    pi = main_ring.next()
                    for kc in range(KC):
                        kb.op("pe", MM(PS[pi][:, 0:TW], wA[ia][:, kc, jj * 128:(jj + 1) * 128], uT[:, kc, :],
                                       kc == 0, kc == KC - 1), reads=[BwA[ia], BuT], writes=[BPS[pi]])
                    consumer(ci, pi)
            pipe.add(load, compute)

    pipeA = Pipe(2)
    for t in range(NTILE):
        ffn_items(pipeA, t, "f1in", "f1out", 0, 1)
        pipeA.add(None, (lambda t=t: make_u(t, 2)))

        def qkv_consumer(ci, pi, t=t):
            kind, hp = ci // 4, ci % 4
            si = st_ring.next()
            if kind == 0:
                kb.op("dve", TS(stage[si][:, :], PS[pi][:, 0:TW], 0.125, None, ALU.mult),
                      reads=[BPS[pi]], writes=[Bstage[si]])
            else:
                kb.op("act", ACTF(stage[si][:, :], PS[pi][:, 0:TW], AF.Copy), reads=[BPS[pi]],
                      writes=[Bstage[si]])
            dst = S1[kind][2 * hp:2 * hp + 2, :, t * TW:(t + 1) * TW].rearrange("h d t -> (h d) t")
            kb.dma("sp", dst, stage[si][:, :], reads=[Bstage[si]], writes=[BS1])
        proj_items(pipeA, "win", 0, 12, qkv_consumer)

        def f_load(t=t):
            kb.dma("sp", wf8[:, :, :], wview("win")[:, :, 1536:1544], reads=[Bwfull["win"]], writes=[Bwf8])

        def f_compute(t=t):
            pi = aux_ring.next()
            for kc in range(KC):
                kb.op("pe", MM(PS[pi][0:H, 0:TW], wf8[:, kc, :], uT[:, kc, :], kc == 0, kc == KC - 1),
                      reads=[Bwf8, BuT], writes=[BPS[pi]])
            kb.op("act", ACTF(lfT[:, :], PS[pi][0:H, 0:TW], AF.Exp, bias=nbf[:, 0:1], scale=-1.0),
                  reads=[BPS[pi], Bconst], writes=[BlfT])
            kb.op("act", ACTF(lfT[:, :], lfT[:, :], AF.Ln, bias=onet[0:H, 0:1], scale=1.0),
                  reads=[BlfT, Bconst], writes=[BlfT])
            kb.op("dve", TS(lfT[:, :], lfT[:, :], -1.0, None, ALU.mult), reads=[BlfT], writes=[BlfT])
            kb.dma("sp", SF[:, t * TW:(t + 1) * TW], lfT[:, :], reads=[BlfT], writes=[BSF])
        pipeA.add(f_load, f_compute)
    if stop is not None and stop.startswith("A:"):
        pipeA.run(int(stop[2:]))
        return finish()
    pipeA.run()

    if stop == "A":
        return finish()
    for k in range(3):
        kb.collective([S1[k].rearrange("h d t -> (h d) t")], [G1[k].rearrange("r h d t -> (r h d) t")],
                      reads=[BS1], writes=[BG1])
    kb.collective([SF], [GF.rearrange("r h t -> (r h) t")], reads=[BSF], writes=[BGF])
    kb.barrier()

    pid_cache = {}

    def dynsrc(eng, fn):
        if "pid" not in pid_cache:
            pid_cache["pid"] = eng.partition_id()
        return fn(pid_cache["pid"])

    def dma_dyn(out, fn, reads, writes):
        idx = kb._next_dma_idx("sp")
        prev = kb.dma_val[idx]
        ex = [(("dma", idx), prev)] if prev else []
        waits = kb._deps("sp", reads, writes, ex)
        kb.dma_val[idx] = prev + 16
        tok = (("dma", idx), prev + 16)
        sem = kb.dma_sems[idx]

        def thunk(eng, waits=waits, sem=sem, out=out, fn=fn):
            for s, v in waits:
                eng.wait_ge(s, v)
            src = dynsrc(eng, fn)
            try:
                eng.dma_start(out=out, in_=src).then_inc(sem, 16)
            except Exception:
                print("DYN DMA FAIL", out, src)
                raise
        kb.prog["sp"].append(thunk)
        kb._commit(tok, reads, writes)
        return tok

    for k in (1, 0, 2):
        dma_dyn(L1[k], lambda pid, k=k: G1[k][:, bass.ds(pid, 1), :, :].rearrange("r o d t -> r (o d) t"),
                [BG1], [BL1])
    dma_dyn(LF, lambda pid: GF[:, bass.ds(pid, 1), :].rearrange("r o t -> r (o t)"), [BGF], [BLF])

    s_ring = Ring([0, 1, 2, 3])
    pt_ring = Ring([0, 1, 2, 3])
    acc_ring = Ring([4, 5])
    o_ring = Ring([0, 1])

    kb.op("pool", MEMSET(Vaug[:, :, 64:128], 1.0), writes=[BVaug])
    for b in range(2):
        r0 = 4 * b
        kb.op("pool", MEMSET(Qaug[0:32, :], 0.0), writes=[BQaug])
        kb.op("pool", MEMSET(Kaug[0:32, :], 0.0), writes=[BKaug])
        kb.op("pool", MEMSET(Kaug[0:2, :], 1.0), writes=[BKaug])
        kb.dma("sp", Kaug[32:96, 0:NMETA], L1[1][r0, :, 0:NMETA], reads=[BL1], writes=[BKaug])
        for i in range(4):
            kb.dma("sp", Kaug[32:96, NMETA + i * TL: NMETA + (i + 1) * TL], L1[1][r0 + i, :, EX:NT],
                   reads=[BL1], writes=[BKaug])
            kb.dma("sp", Qaug[32:96, i * TL:(i + 1) * TL], L1[0][r0 + i, :, EX:NT], reads=[BL1], writes=[BQaug])
        kb.dma("sp", VTst[0:64, 0:NMETA], L1[2][r0, :, 0:NMETA], reads=[BL1], writes=[BVTst])
        pi = aux_ring.next()
        kb.op("pe", MM(PS[pi][0:NMETA, 0:64], VTst[0:64, 0:NMETA], Ibf[0:64, 0:64], True, True),
              reads=[BVTst, Bconst], writes=[BPS[pi]])
        kb.op("dve", CP(Vaug[0:NMETA, 0, 0:64], PS[pi][0:NMETA, 0:64]), reads=[BPS[pi]], writes=[BVaug])
        for i in range(4):
            kb.dma("sp", VTst[0:64, :], L1[2][r0 + i, :, EX:NT], reads=[BL1], writes=[BVTst])
            for g in range(2):
                pi = aux_ring.next()
                for q in range(8):
                    blk = g * 8 + q
                    kb.op("pe", MM(PS[pi][:, q * 64:(q + 1) * 64], VTst[0:64, blk * 128:(blk + 1) * 128],
                                   Ibf[0:64, 0:64], True, True), reads=[BVTst, Bconst], writes=[BPS[pi]])
                kb.op("dve", CP(Vaug[:, 1 + i * 16 + g * 8: 1 + i * 16 + g * 8 + 8, 0:64],
                                PS[pi][:, :].rearrange("p (q c) -> p q c", q=8)),
                      reads=[BPS[pi]], writes=[BVaug])
        kb.op("dve", MEMSET(lfR[:, :], 0.0), writes=[BlfR])
        kb.dma("sp", lfR[0:1, 0:NMETA], LF[r0:r0 + 1, 0:NMETA], reads=[BLF], writes=[BlfR])
        for i in range(4):
            kb.dma("sp", lfR[1 + 16 * i: 17 + 16 * i, :],
                   LF[r0 + i:r0 + i + 1, EX:NT].rearrange("o (a c) -> (o a) c", c=128), reads=[BLF], writes=[BlfR])
        pi = aux_ring.next()
        kb.op("pe", MM(PS[pi][:, 0:NBLK], lfR[:, :], I32[0:NBLK, 0:NBLK], True, True),
              reads=[BlfR, Bconst], writes=[BPS[pi]])
        kb.op("dve", CP(lfC[:, :], PS[pi][:, 0:NBLK]), reads=[BPS[pi]], writes=[BlfC])
        pi = aux_ring.next()
        kb.op("pe", MM(PS[pi][0:NBLK, 0:128], lfC[:, :], ONE32, True, True), reads=[BlfC, Bconst],
              writes=[BPS[pi]])
        kb.op("dve", CP(totT[:, :], PS[pi][0:NBLK, 0:128]), reads=[BPS[pi]], writes=[BtotT])
        pi = aux_ring.next()
        kb.op("pe", MM(PS[pi][:, 0:NBLK], U32, lfC[:, :], True, False), reads=[BlfC, Bconst], writes=[BPS[pi]])
        kb.op("pe", MM(PS[pi][:, 0:NBLK], totT[:, :], SU32[0:NBLK, 0:NBLK], False, True),
              reads=[BtotT, Bconst], writes=[BPS[pi]])
        kb.op("dve", TS(negF[:, :], PS[pi][:, 0:NBLK], -1.0, None, ALU.mult), reads=[BPS[pi]], writes=[BnegF])
        kb.op("dve", CP(Ff[:, :], PS[pi][:, 0:NBLK]), reads=[BPS[pi]], writes=[BFf])
        kb.op("dve", CP(Fp[:, :, 0], Ff[:, :]), reads=[BFf], writes=[BFp])
        kb.op("dve", TT(Fp[:, :, 1], Ff[:, :], Fp[:, :, 0], ALU.subtract), reads=[BFf, BFp], writes=[BFp])
        for g in range(16):
            pi = aux_ring.next()
            for q in range(4):
                rb = g * 4 + q
                kb.op("pe", MM(PS[pi][0:2, q * 128:(q + 1) * 128], Fp[:, rb + 1, :], Ibf, True, True),
                      reads=[BFp, Bconst], writes=[BPS[pi]])
            kb.op("dve", CP(Qaug[0:2, g * 512:(g + 1) * 512], PS[pi][0:2, :]), reads=[BPS[pi]], writes=[BQaug])

        steps = []
        for qc in range(16):
            blocks = [(-1, 0)] + [(rb, 0) for rb in range(4 * qc)] + [(4 * qc + r, r) for r in range(4)]
            for bi, (rb, r) in enumerate(blocks):
                steps.append((qc, bi, len(blocks), rb, r))
        LA = 2
        st = {}

        def s_part(i):
            qc, bi, nb, rb, r = steps[i]
            if bi == 0:
                st[("acc", qc)] = acc_ring.next()
            q0 = qc * 512
            if rb < 0:
                nk, kcol, vb = NMETA, 0, 0
            else:
                nk, kcol, vb = 128, NMETA + rb * 128, rb + 1
            diag = rb >= 4 * qc
            N = 512 - 128 * r
            si = s_ring.next()
            kb.op("pe", MM(PS[si][0:nk, 0:N], Kaug[0:96, kcol:kcol + nk], Qaug[0:96, q0 + r * 128: q0 + 512],
                           True, not diag), reads=[BKaug, BQaug], writes=[BPS[si]])
            if diag:
                kb.op("pe", MM(PS[si][:, 0:128], Ibf, TRIbf, False, True), reads=[Bconst], writes=[BPS[si]])
            pti = pt_ring.next()
            kb.op("act", ACTF(PT[pti][0:nk, 0:N], PS[si][0:nk, 0:N], AF.Exp, bias=negF[0:nk, vb:vb + 1]),
                  reads=[BPS[si], BnegF], writes=[BPT[pti]])
            st[i] = (pti, nk, vb, N)

        def pv_part(i):
            qc, bi, nb, rb, r = steps[i]
            pti, nk, vb, N = st.pop(i)
            ai = st[("acc", qc)]
            kb.op("pe", MM(PS[ai][:, r * 128:512], Vaug[0:nk, vb, :], PT[pti][0:nk, 0:N],
                           bi == 0, bi == nb - 1, inc=True), reads=[BVaug, BPT[pti]], writes=[BPS[ai]])
            if bi == nb - 1:
                oi = o_ring.next()
                kb.op("dve", CP(osb[oi][:, :], PS[ai][:, :]), reads=[BPS[ai]], writes=[Bosb[oi]])
                kb.dma("sp", den[oi][:, :], osb[oi][64:128, :], reads=[Bosb[oi]], writes=[Bden[oi]])
                kb.op("dve", RECIP(den[oi][:, :], den[oi][:, :]), reads=[Bden[oi]], writes=[Bden[oi]])
                kb.op("dve", TT(abf[oi][:, :], osb[oi][0:64, :], den[oi][:, :], ALU.mult),
                      reads=[Bosb[oi], Bden[oi]], writes=[Babf[oi]])
                tcid = 4 * b + qc // 4
                kb.dma("sp", S2[tcid, :, (qc % 4) * 512:(qc % 4 + 1) * 512], abf[oi][:, :], reads=[Babf[oi]],
                       writes=[BS2])

        nst = len(steps)
        for i in range(nst + LA):
            if i < nst:
                s_part(i)
            if i - LA >= 0:
                pv_part(i - LA)

    if stop == "B":
        return finish()
    kb.collective([S2.rearrange("c d t -> (c d) t")], [G2.rearrange("h c d t -> (h c d) t")],
                  reads=[BS2], writes=[BG2])
    kb.barrier()

    dma_dyn(L2, lambda pid: G2[:, bass.ds(pid, 1), :, :].rearrange("h o d t -> h (o d) t"), [BG2], [BL2])
    kb.op("pool", MEMSET(ccbuf[:, :, 0:2], 0.0), writes=[Bcc])
    pipeC = Pipe(2)
    yv = yTf[:, :].rearrange("p (k c) -> p k c", k=KC)
    for t in range(NTILE):
        c0 = t * TW
        pipeC.add(None, (lambda t=t: make_u(t, 2)))

        def attn_load(t=t):
            a0 = EX if t == 0 else 0
            tl0 = t * TW + a0 - EX
            if t == 0:
                kb.op("pool", MEMSET(attnT[:, :, 0:EX], 0.0), writes=[BattnT])
            for wi, wname in enumerate(("wab", "wcb")):
                kb.dma("sp", wBr[wi], wfull[wname].rearrange("(kc p) n -> p kc n", p=128),
                       reads=[Bwfull[wname]], writes=[BwO[wi]])
            for h in range(H):
                kb.dma("sp", attnT[(h % 2) * 64:(h % 2) * 64 + 64, h // 2, a0:TW], L2[h, :, tl0:tl0 + TW - a0],
                       reads=[BL2], writes=[BattnT])

        def conv_consumer(ci, pi, t=t):
            grp, ch = ci // 4, ci % 4
            if grp == 0:
                kb.op("act", ACTF(cbT[:, ch, :], PS[pi][:, 0:TW], AF.Copy), reads=[BPS[pi]], writes=[BcbT])
            elif grp == 1:
                kb.op("act", ACTF(ccbuf[:, ch, 2:2 + TW], PS[pi][:, 0:TW], AF.Copy), reads=[BPS[pi]],
                      writes=[Bcc])
            else:
                kb.op("dve", TT(ccbuf[:, ch, 2:2 + TW], ccbuf[:, ch, 2:2 + TW], PS[pi][:, 0:TW], ALU.mult),
                      reads=[BPS[pi], Bcc], writes=[Bcc])
                ti = tmp_ring.next()
                kb.op("dve", TS(tmp[ti][:, :], ccbuf[:, ch, 0:TW], cw[:, 0 * 4 + ch: 0 * 4 + ch + 1], None, ALU.mult),
                      reads=[Bcc, Bconst], writes=[Btmp[ti]])
                kb.op("dve", STT(tmp[ti][:, :], ccbuf[:, ch, 1:1 + TW], cw[:, 1 * 4 + ch: 1 * 4 + ch + 1],
                                 tmp[ti][:, :], ALU.mult, ALU.add), reads=[Bcc, Bconst, Btmp[ti]], writes=[Btmp[ti]])
                kb.op("dve", STT(tmp[ti][:, :], ccbuf[:, ch, 2:2 + TW], cw[:, 2 * 4 + ch: 2 * 4 + ch + 1],
                                 tmp[ti][:, :], ALU.mult, ALU.add), reads=[Bcc, Bconst, Btmp[ti]], writes=[Btmp[ti]])
                kb.op("dve", TT(convg[:, ch, :], tmp[ti][:, :], cbT[:, ch, :], ALU.mult),
                      reads=[Btmp[ti], BcbT], writes=[Bconvg])
                kb.op("dve", CP(ccbuf[:, ch, 0:2], ccbuf[:, ch, TW:TW + 2]), reads=[Bcc], writes=[Bcc])
        pipeC.add(attn_load, lambda: None)
        proj_items(pipeC, "win", 1544, 12, conv_consumer)

        def gate_consumer(ci, pi, t=t):
            which, dc = ci // 8, ci % 8
            src = attnT if which == 0 else convg
            Bsrc = BattnT if which == 0 else Bconvg
            gi = sig_ring.next()
            kb.op("act", ACTF(sig[gi][:, :], PS[pi][:, 0:TW], AF.Sigmoid), reads=[BPS[pi]], writes=[Bsig[gi]])
            p2 = main_ring.next()
            for kc in range(4):
                kb.op("pe", MM(PS[p2][:, 0:TW], wBr[which][:, kc, dc * 128:(dc + 1) * 128], src[:, kc, :],
                               kc == 0, kc == 3), reads=[BwO[which], Bsrc], writes=[BPS[p2]])
            if which == 0:
                kb.op("dve", TT(yv[:, dc, :], sig[gi][:, :], PS[p2][:, 0:TW], ALU.mult),
                      reads=[Bsig[gi], BPS[p2]], writes=Byt)
            else:
                ti = tmp_ring.next()
                kb.op("dve", TT(tmp[ti][:, :], sig[gi][:, :], PS[p2][:, 0:TW], ALU.mult),
                      reads=[Bsig[gi], BPS[p2]], writes=[Btmp[ti]])
                kb.op("dve", TT(gT[:, dc, :], tmp[ti][:, :], yv[:, dc, :], ALU.add),
                      reads=[Btmp[ti]] + Byt, writes=[BgT])
        proj_items(pipeC, "win", 3080, 16, gate_consumer)

        def wout_items(t=t):
            wv = wview("wout")
            for g in range(4):
                bufs = {}

                def load(g=g, bufs=bufs):
                    ia = wA_ring.next()
                    bufs["a"] = ia
                    kb.dma("sp", wA[ia], wv[:, :, g * 256:(g + 1) * 256], reads=[Bwfull["wout"]], writes=[BwA[ia]])

                def compute(g=g, bufs=bufs):
                    ia = bufs["a"]
                    for jj in range(2):
                        dc = g * 2 + jj
                        pi = main_ring.next()
                        for kc in range(KC):
                            kb.op("pe", MM(PS[pi][:, 0:TW], wA[ia][:, kc, jj * 128:(jj + 1) * 128], gT[:, kc, :],
                                           kc == 0, kc == KC - 1), reads=[BwA[ia], BgT], writes=[BPS[pi]])
                        kb.op("act", ACTF(yv[:, dc, :], PS[pi][:, 0:TW], AF.Copy), reads=[BPS[pi]], writes=Byt)
                pipeC.add(load, compute)
        wout_items()
        pipeC.add(None, (lambda t=t: post_norm_add(t, lambda kc: gn[:, 3 * 8 + kc: 3 * 8 + kc + 1])))
        ffn_items(pipeC, t, "f2in", "f2out", 4, 5)

        def out_store(t=t):
            c0 = t * TW
            a0 = EX if t == 0 else 0
            col = c0 + a0
            while col < c0 + TW:
                n = min(128, c0 + TW - col)
                xi = (col // 128) % 2
                xb, Bx = xs[xi], Byt[xi]
                for half in range(2):
                    pi = main_ring.next()
                    for q in range(4):
                        kc = half * 4 + q
                        kb.op("pe", MM(PS[pi][0:n, q * 128:(q + 1) * 128], hT[:, kc, col:col + n], I32,
                                       True, True), reads=[BhT[t], Bconst], writes=[BPS[pi]])
                    eng = cpe.next()
                    if eng == "dve":
                        kb.op("dve", CP(xb[0:n, half * 512:(half + 1) * 512], PS[pi][0:n, :]), reads=[BPS[pi]],
                              writes=[Bx])
                    else:
                        kb.op("act", ACTF(xb[0:n, half * 512:(half + 1) * 512], PS[pi][0:n, :], AF.Copy),
                              reads=[BPS[pi]], writes=[Bx])
                kb.dma("sp", y_out[col - EX: col - EX + n, :], xb[0:n, :], reads=[Bx])
                col += n
        pipeC.add(None, out_store)
    if stop is not None and stop.startswith("C:"):
        pipeC.run(int(stop[2:]))
        return finish()
    pipeC.run()

    toks = kb.all_tokens()
    waits = kb._deps("sp", (), (), toks)

    def fin(eng, waits=waits):
        for s, v in waits:
            eng.wait_ge(s, v)
    kb.prog["sp"].append(fin)
    kb.emit()
    return nc


PELT = "dve"
_NC_CACHE = {}
_STOP = None


def _consts():
    c = np.zeros((128, 640), np.float32)
    i = np.arange(128)
    c[:, 0:128] = np.eye(128, dtype=np.float32)
    c[:, 128:256] = (i[:, None] <= i[None, :]).astype(np.float32)
    c[:, 256:384] = (i[:, None] < i[None, :]).astype(np.float32)
    c[:, 384:512] = 1.0
    c[:, 512:640] = (i[:, None] > i[None, :]).astype(np.float32) * -30000.0
    return c


def kernel(x, meta_tokens, w_in, b_forget, conv_w, w_attn_branch, w_conv_branch, w_out,
           g_ffn1_pre, g_ffn1_post, w_ffn1_in, w_ffn1_out,
           g_mix_pre, g_mix_post, g_ffn2_pre, g_ffn2_post, w_ffn2_in, w_ffn2_out):
    f = lambda a: np.ascontiguousarray(np.asarray(a, dtype=np.float32))
    x = f(x)
    meta = f(meta_tokens)
    ws = {"f1in": f(w_ffn1_in)[0], "f1out": f(w_ffn1_out)[0], "win": f(w_in)[0], "wab": f(w_attn_branch)[0],
          "wcb": f(w_conv_branch)[0], "wout": f(w_out)[0], "f2in": f(w_ffn2_in)[0], "f2out": f(w_ffn2_out)[0]}
    gains = [f(g)[0] for g in (g_ffn1_pre, g_ffn1_post, g_mix_pre, g_mix_post, g_ffn2_pre, g_ffn2_post)]
    gn = np.zeros((128, 48), np.float32)
    for i, g in enumerate(gains):
        gn[:, i * 8:(i + 1) * 8] = g.reshape(8, 128).T
    cwm = np.zeros((128, 12), np.float32)
    cwv = f(conv_w)[0]
    for j in range(3):
        cwm[:, j * 4:(j + 1) * 4] = cwv[j].reshape(4, 128).T
    bfg = f(b_forget)[0].reshape(8, 1)
    cst = _consts()

    if "nc" not in _NC_CACHE:
        _NC_CACHE["nc"] = build_nc(_STOP)
    nc = _NC_CACHE["nc"]

    in_maps = []
    for c in range(NCORES):
        b, j = c // 4, c % 4
        xe = np.zeros((EX, D), np.float32)
        xe[0:NMETA] = meta
        xe[EX - 2:EX] = meta[NMETA - 2:NMETA] if j == 0 else x[b, j * TL - 2: j * TL]
        m = {"x": np.ascontiguousarray(x[b, j * TL:(j + 1) * TL]), "xe": xe, "gn": gn, "cw": cwm,
             "bfg": bfg, "cst": cst}
        for name, K, N in WSPEC:
            rows = K // NCORES
            m["w_" + name] = np.ascontiguousarray(ws[name][c * rows:(c + 1) * rows])
        in_maps.append(m)
    res = run_bass_kernel_spmd(nc, in_maps, core_ids=list(range(NCORES)))
    out = np.zeros((2, SEQ, D), np.float32)
    for c in range(NCORES):
        b, j = c // 4, c % 4
        out[b, j * TL:(j + 1) * TL] = res.results[c]["y"]
    return out
```

```python
import numpy as np
import concourse.bass as bass
import concourse.mybir as mybir
from concourse.bass_utils import run_bass_kernel_spmd

F32 = mybir.dt.float32
BF16 = mybir.dt.bfloat16
AF = mybir.ActivationFunctionType
ALU = mybir.AluOpType

NCORES = 8
D = 1024
KC = 8
SEQ = 8192
NMETA = 16
TL = 2048
EX = 32
NT = TL + EX
TW = 416
NTILE = NT // TW
DFF = 2816
FC = DFF // 128
INC = 5128
H = 8
DH = 64
EPS = 1e-6
NBLK = 65
NDMA_SEMS = 28


class Buf:
    __slots__ = ("name", "last_w", "reads")

    def __init__(self, name):
        self.name = name
        self.last_w = None
        self.reads = []


class KB:
    ENGS = ("pe", "act", "dve", "pool", "sp")

    def __init__(self, nc):
        self.nc = nc
        self.sem = {e: nc.alloc_semaphore("sem_" + e) for e in self.ENGS}
        self.cnt = {e: 0 for e in self.ENGS}
        self.waited = {e: {} for e in self.ENGS}
        self.prog = {e: [] for e in self.ENGS}
        self.dma_sems = [nc.alloc_semaphore("dsem%d" % i) for i in range(NDMA_SEMS)]
        self.dma_val = [0] * NDMA_SEMS
        self.dma_next = 0
        self.cc_sem = nc.alloc_semaphore("ccsem")
        self.cc_val = 0

    def _semh(self, key):
        if key[0] == "eng":
            return self.sem[key[1]]
        if key[0] == "dma":
            return self.dma_sems[key[1]]
        return self.cc_sem

    def _deps(self, engine, reads, writes, extra=()):
        deps = {}

        def add(tok):
            if tok is None:
                return
            k, v = tok
            if deps.get(k, 0) < v:
                deps[k] = v
        for r in reads:
            add(r.last_w)
        for w in writes:
            add(w.last_w)
            for t in w.reads:
                add(t)
        for t in extra:
            add(t)
        waits = []
        for k, v in deps.items():
            if k == ("eng", "pe") and engine == "pe":
                continue
            if self.waited[engine].get(k, 0) >= v:
                continue
            self.waited[engine][k] = v
            waits.append((self._semh(k), v))
        return waits

    def _commit(self, tok, reads, writes):
        for r in reads:
            r.reads.append(tok)
            if len(r.reads) > 64:
                best = {}
                for k, v in r.reads:
                    if best.get(k, 0) < v:
                        best[k] = v
                r.reads = list(best.items())
        for w in writes:
            w.last_w = tok
            w.reads = []

    def op(self, engine, fn, reads=(), writes=(), extra=(), inc=True):
        if isinstance(fn, MMop):
            inc = fn.stop
        waits = self._deps(engine, reads, writes, extra)
        if inc:
            self.cnt[engine] += 1
            tok = (("eng", engine), self.cnt[engine])
        else:
            tok = (("eng", engine), self.cnt[engine] + 1)
        sem = self.sem[engine]

        def thunk(eng, waits=waits, fn=fn, sem=sem, inc=inc):
            for s, v in waits:
                eng.wait_ge(s, v)
            ins = fn(eng)
            if inc:
                ins.then_inc(sem, 1)
        self.prog[engine].append(thunk)
        self._commit(tok, reads, writes)
        return tok

    def _next_dma_idx(self, engine):
        if engine == "pool":
            self.pool_next = (getattr(self, "pool_next", -1) + 1) % 4
            return NDMA_SEMS - 4 + self.pool_next
        idx = self.dma_next
        self.dma_next = (self.dma_next + 1) % (NDMA_SEMS - 4)
        return idx

    def dma(self, engine, out, in_, reads=(), writes=(), extra=()):
        idx = self._next_dma_idx(engine)
        prev = self.dma_val[idx]
        ex = list(extra)
        if prev:
            ex.append((("dma", idx), prev))
        waits = self._deps(engine, reads, writes, ex)
        self.dma_val[idx] = prev + 16
        tok = (("dma", idx), prev + 16)
        sem = self.dma_sems[idx]

        def thunk(eng, waits=waits, sem=sem, out=out, in_=in_):
            for s, v in waits:
                eng.wait_ge(s, v)
            eng.dma_start(out=out, in_=in_).then_inc(sem, 16)
        self.prog[engine].append(thunk)
        self._commit(tok, reads, writes)
        return tok

    def collective(self, ins, outs, reads=(), writes=()):
        waits = self._deps("pool", reads, writes)
        self.cc_val += 1
        tok = (("cc", 0), self.cc_val)
        sem = self.cc_sem
        groups = [list(range(NCORES))]

        def thunk(eng, waits=waits):
            for s, v in waits:
                eng.wait_ge(s, v)
            eng.collective_compute("AllGather", ALU.bypass, replica_groups=groups,
                                   ins=ins, outs=outs).then_inc(sem)
        self.prog["pool"].append(thunk)
        self._commit(tok, reads, writes)
        return tok

    def all_tokens(self):
        toks = [(("eng", e), self.cnt[e]) for e in self.ENGS if self.cnt[e]]
        toks += [(("dma", i), self.dma_val[i]) for i in range(NDMA_SEMS) if self.dma_val[i]]
        if self.cc_val:
            toks.append((("cc", 0), self.cc_val))
        return toks

    def barrier(self):
        toks = self.all_tokens()
        for e in self.ENGS:
            waits = self._deps(e, (), (), [t for t in toks if t[0] != ("eng", e)])

            def thunk(eng, waits=waits):
                for s, v in waits:
                    eng.wait_ge(s, v)
            self.prog[e].append(thunk)

    def emit(self):
        nc = self.nc
        with nc.Block() as block:
            @block.tensor
            def _(e):
                for t in self.prog["pe"]:
                    t(e)

            @block.scalar
            def _(e):
                for t in self.prog["act"]:
                    t(e)

            @block.vector
            def _(e):
                for t in self.prog["dve"]:
                    t(e)

            @block.gpsimd
            def _(e):
                for t in self.prog["pool"]:
                    t(e)

            @block.sync
            def _(e):
                for t in self.prog["sp"]:
                    t(e)


class MMop:
    def __init__(self, out, lhsT, rhs, start, stop, inc=None):
        self.a = (out, lhsT, rhs, start, stop)
        self.stop = stop if inc is None else inc

    def __call__(self, e):
        out, lhsT, rhs, start, stop = self.a
        return e.matmul(out, lhsT=lhsT, rhs=rhs, start=start, stop=stop, skip_group_check=True)


def MM(out, lhsT, rhs, start, stop, inc=None):
    return MMop(out, lhsT, rhs, start, stop, inc)


def ACTF(out, in_, func, bias=None, scale=1.0):
    if bias is None:
        return lambda e: e.activation(out=out, in_=in_, func=func, scale=scale)
    return lambda e: e.activation(out=out, in_=in_, func=func, bias=bias, scale=scale)


def CP(out, in_):
    return lambda e: e.tensor_copy(out=out, in_=in_)


def TT(out, in0, in1, op):
    return lambda e: e.tensor_tensor(out=out, in0=in0, in1=in1, op=op)


def TS(out, in0, s1, s2, op0, op1=None):
    if op1 is None:
        return lambda e: e.tensor_scalar(out=out, in0=in0, scalar1=s1, scalar2=None, op0=op0)
    return lambda e: e.tensor_scalar(out=out, in0=in0, scalar1=s1, scalar2=s2, op0=op0, op1=op1)


def STT(out, in0, scalar, in1, op0, op1):
    return lambda e: e.scalar_tensor_tensor(out=out, in0=in0, scalar=scalar, in1=in1, op0=op0, op1=op1)


def MEMSET(ap, val):
    return lambda e: e.memset(ap, val)


def RECIP(out, in_):
    return lambda e: e.reciprocal(out=out, in_=in_)


class Ring:
    def __init__(self, items):
        self.items = items
        self.i = 0

    def next(self):
        it = self.items[self.i % len(self.items)]
        self.i += 1
        return it


WSPEC = [
    ("f1in", D, 2 * DFF), ("f1out", DFF, D), ("win", D, INC), ("wab", 512, D), ("wcb", 512, D),
    ("wout", D, D), ("f2in", D, 2 * DFF), ("f2out", DFF, D),
]


def build_nc(stop=None):
    nc = bass.Bass("TRN2", target_bir_lowering=False)
    nc.allow_low_precision("bf16 matmul operands with fp32 PSUM accumulation")
    kb = KB(nc)

    x_in = nc.dram_tensor("x", [TL, D], F32, kind="ExternalInput").ap()
    xe_in = nc.dram_tensor("xe", [EX, D], F32, kind="ExternalInput").ap()
    gn_in = nc.dram_tensor("gn", [128, 48], F32, kind="ExternalInput").ap()
    cw_in = nc.dram_tensor("cw", [128, 12], F32, kind="ExternalInput").ap()
    bfg_in = nc.dram_tensor("bfg", [H, 1], F32, kind="ExternalInput").ap()
    cst_in = nc.dram_tensor("cst", [128, 640], F32, kind="ExternalInput").ap()
    y_out = nc.dram_tensor("y", [TL, D], F32, kind="ExternalOutput").ap()
    wsh, wsend, wfull, Bwsend, Bwfull = {}, {}, {}, {}, {}
    for name, K, N in WSPEC:
        wsh[name] = nc.dram_tensor("w_" + name, [K // NCORES, N], F32, kind="ExternalInput").ap()
        wsend[name] = nc.dram_tensor("ws_" + name, [K // NCORES, N], BF16).ap()
        wfull[name] = nc.dram_tensor("wf_" + name, [K, N], BF16).ap()
        Bwsend[name] = Buf("ws_" + name)
        Bwfull[name] = Buf("wf_" + name)
    S1 = [nc.dram_tensor("S1_%d" % t, [3, H, DH, TW], BF16).ap() for t in range(NTILE)]
    G1 = [nc.dram_tensor("G1_%d" % t, [NCORES, 3, H, DH, TW], BF16).ap() for t in range(NTILE)]
    SF = [nc.dram_tensor("SF_%d" % t, [H, TW], F32).ap() for t in range(NTILE)]
    GF = [nc.dram_tensor("GF_%d" % t, [NCORES, H, TW], F32).ap() for t in range(NTILE)]
    BS1t = [Buf("S1_%d" % t) for t in range(NTILE)]
    BG1t = [Buf("G1_%d" % t) for t in range(NTILE)]
    BSFt = [Buf("SF_%d" % t) for t in range(NTILE)]
    BGFt = [Buf("GF_%d" % t) for t in range(NTILE)]
    S2 = nc.dram_tensor("S2", [NCORES, DH, TL], BF16).ap()
    G2 = nc.dram_tensor("G2", [H, NCORES, DH, TL], BF16).ap()
    BS1, BG1, BSF, BGF, BS2, BG2 = [Buf(n) for n in ("S1", "G1", "SF", "GF", "S2", "G2")]
    L1all = nc.dram_tensor("L1all", [NCORES, 3, DH, NT], BF16).ap()
    LF = nc.dram_tensor("LF", [NCORES, NT], F32).ap()
    L2 = nc.dram_tensor("L2", [H, DH, TL], BF16).ap()
    BL1, BLF, BL2 = [Buf(n) for n in ("L1", "LF", "L2")]

    PS = [nc.alloc_psum_tensor("ps%d" % i, [128, 512], F32) for i in range(8)]
    BPS = [Buf("ps%d" % i) for i in range(8)]

    def sb(name, shape, dt):
        return nc.alloc_sbuf_tensor("sb_" + name, shape, dt)

    hT = sb("hT", [128, KC, NT], F32)
    BhT = [Buf("hT%d" % t) for t in range(NTILE)]
    AR = 28800
    arena = sb("arena", [128, AR], BF16)
    uT = sb("uT", [128, KC, TW], BF16)
    BuT = Buf("uT")
    yTf = sb("yT", [128, KC * TW], F32)
    Byt = [Buf("yt%d" % i) for i in range(3)]
    sa = [sb("sa%d" % i, [128, TW], F32) for i in range(2)]
    Bsa = [Buf("sa%d" % i) for i in range(2)]
    tmp = [sb("tmp%d" % i, [128, TW], F32) for i in range(2)]
    Btmp = [Buf("tmp%d" % i) for i in range(2)]
    rstd = sb("rstd", [128, TW], F32)
    Brstd = Buf("rstd")
    sq = [sb("sq%d" % i, [128, TW], BF16) for i in range(2)]
    Bsq = [Buf("sq%d" % i) for i in range(2)]
    stage = [sb("stage%d" % i, [128, TW], BF16) for i in range(3)]
    Bstage = [Buf("stage%d" % i) for i in range(3)]
    lfT = sb("lfT", [H, TW], F32)
    BlfT = Buf("lfT")
    wf8 = sb("wf8", [128, KC, 8], BF16)
    Bwf8 = Buf("wf8")
    gn = sb("gn", [128, 48], F32)
    gh = sb("gh", [128, 48], F32)
    cw = sb("cw", [128, 12], F32)
    bfg = sb("bfg", [H, 1], F32)
    nbf = sb("nbf", [H, 1], F32)
    cst = sb("cst", [128, 640], F32)
    cbf = sb("cbf", [128, 384], BF16)
    epst = sb("epst", [128, 1], F32)
    onet = sb("onet", [128, 1], F32)
    Bconst = Buf("const")
    ccbuf = sb("ccbuf", [128, 4, TW + 2], F32)
    Bcc = Buf("cc")
    cbT = sb("cbT", [128, 4, TW], F32)
    BcbT = Buf("cbT")
    convg = sb("convg", [128, 4, TW], BF16)
    Bconvg = Buf("convg")
    attnT = sb("attnT", [128, 4, TW], BF16)
    BattnT = Buf("attnT")
    sig = [sb("sig%d" % i, [128, TW], F32) for i in range(2)]
    Bsig = [Buf("sig%d" % i) for i in range(2)]
    PT = [sb("PT%d" % i, [128, 512], BF16) for i in range(4)]
    BPT = [Buf("PT%d" % i) for i in range(4)]
    osb = [sb("osb%d" % i, [128, 512], F32) for i in range(2)]
    Bosb = [Buf("osb%d" % i) for i in range(2)]
    den = [sb("den%d" % i, [64, 512], F32) for i in range(2)]
    Bden = [Buf("den%d" % i) for i in range(2)]
    abf = [sb("abf%d" % i, [64, 512], BF16) for i in range(2)]
    Babf = [Buf("abf%d" % i) for i in range(2)]
    lfR = sb("lfR", [NBLK, 128], F32)
    lfC = sb("lfC", [128, NBLK], F32)
    totT = sb("totT", [NBLK, 128], F32)
    negF = sb("negF", [128, NBLK], F32)
    Ff = sb("Ff", [128, NBLK], F32)
    Fp = sb("Fp", [128, NBLK, 2], BF16)
    BlfR, BlfC, BtotT, BnegF, BFf, BFp = [Buf(n) for n in ("lfR", "lfC", "totT", "negF", "Ff", "Fp")]

    def v3(off, k, c):
        return arena[:, off:off + k * c].rearrange("p (k c) -> p k c", k=k)

    gT = v3(0, FC, TW)
    BgT = Buf("gT")
    off = FC * TW
    wA = [v3(off + i * 2048, KC, 256) for i in range(4)]
    BwA = [Buf("wA%d" % i) for i in range(4)]
    off += 4 * 2048
    wO = [v3(off + i * 5632, FC, 256) for i in range(2)]
    BwO = [Buf("wO%d" % i) for i in range(2)]
    wBr = [v3(off + i * 5632, 4, 1024) for i in range(2)]
    off += 2 * 5632
    assert off <= AR
    Qaug = arena[:, 0:SEQ]
    Kaug = arena[:, SEQ:SEQ + SEQ + NMETA]
    o2 = 2 * SEQ + NMETA
    Vaug = v3(o2, NBLK, 128)
    o2 += NBLK * 128
    VTst = arena[:, o2:o2 + TL]
    assert o2 + TL <= AR
    BQaug, BKaug, BVaug, BVTst = [Buf(n) for n in ("Qaug", "Kaug", "Vaug", "VTst")]

    I32 = cst[:, 0:128]
    U32 = cst[:, 128:256]
    SU32 = cst[:, 256:384]
    ONE32 = cst[:, 384:512]
    Ibf = cbf[:, 0:128]
    TRIbf = cbf[:, 128:256]
    ONEbf = cbf[:, 256:384]

    def finish():
        toks = kb.all_tokens()
        waits = kb._deps("sp", (), (), toks)

        def fin(eng, waits=waits):
            for s_, v in waits:
                eng.wait_ge(s_, v)
        kb.prog["sp"].append(fin)
        kb.emit()
        return nc

    kb.dma("sp", cst[:, :], cst_in, writes=[Bconst])
    kb.dma("sp", gn[:, :], gn_in, writes=[Bconst])
    kb.dma("sp", cw[:, :], cw_in, writes=[Bconst])
    kb.dma("sp", bfg[:, :], bfg_in, writes=[Bconst])
    kb.op("dve", CP(cbf[:, 0:128], cst[:, 0:128]), reads=[Bconst], writes=[Bconst])
    kb.op("dve", CP(cbf[:, 128:256], cst[:, 512:640]), reads=[Bconst], writes=[Bconst])
    kb.op("dve", CP(cbf[:, 256:384], cst[:, 384:512]), reads=[Bconst], writes=[Bconst])
    kb.op("dve", TS(gh[:, :], gn[:, :], 0.5, None, ALU.mult), reads=[Bconst], writes=[Bconst])
    kb.op("dve", TS(nbf[:, :], bfg[:, :], -1.0, None, ALU.mult), reads=[Bconst], writes=[Bconst])
    kb.op("dve", MEMSET(epst[:, :], EPS), writes=[Bconst])
    kb.op("dve", MEMSET(onet[:, :], 1.0), writes=[Bconst])

    for name, K, N in WSPEC:
        kb.dma("pool", wsend[name], wsh[name], writes=[Bwsend[name]])
        kb.collective([wsend[name]], [wfull[name]], reads=[Bwsend[name]], writes=[Bwfull[name]])

    xs = [yTf[:, 0:1024], yTf[:, 1024:2048]]
    psr = Ring([4, 5, 6])
    cpe = Ring(["dve", "act"])
    nblk_in = 1 + TL // 128
    for bi in range(nblk_in):
        rows = EX if bi == 0 else 128
        c0 = 0 if bi == 0 else EX + (bi - 1) * 128
        src = xe_in if bi == 0 else x_in[(bi - 1) * 128: bi * 128, :]
        xb = xs[bi % 2]
        Bx = Byt[bi % 2]
        kb.dma("sp", xb[0:rows, :], src, writes=[Bx])
        tt = c0 // TW
        tts = sorted(set([c0 // TW, (c0 + rows - 1) // TW]))
        for half in range(2):
            pi = psr.next()
            for q in range(4):
                kc = half * 4 + q
                kb.op("pe", MM(PS[pi][:, q * 128: q * 128 + rows], xb[0:rows, kc * 128:(kc + 1) * 128],
                               I32[0:rows, 0:rows], True, True), reads=[Bx, Bconst], writes=[BPS[pi]])
            eng = cpe.next()
            outv = hT[:, half * 4:(half + 1) * 4, c0:c0 + rows]
            inv = PS[pi][:, :].rearrange("p (q c) -> p q c", q=4)[:, :, 0:rows]
            if eng == "dve":
                kb.op("dve", CP(outv, inv), reads=[BPS[pi]], writes=[BhT[t] for t in tts])
            else:
                kb.op("act", ACTF(outv, inv, AF.Copy), reads=[BPS[pi]], writes=[BhT[t] for t in tts])

    if stop == "1":
        return finish()
    main_ring = Ring([0, 1, 2, 3, 4, 5])
    aux_ring = Ring([6, 7])
    wA_ring = Ring([0, 1, 2, 3])
    wO_ring = Ring([0, 1])
    sa_ring = Ring([0, 1])
    tmp_ring = Ring([0, 1])
    sq_ring = Ring([0, 1])
    st_ring = Ring([0, 1, 2])
    sig_ring = Ring([0, 1])

    def wview(name):
        return wfull[name].rearrange("(kc p) n -> p kc n", p=128)

    def rms_stats(src_fn, Bsrc, nch):
        pi = aux_ring.next()
        for kc in range(nch):
            si = sq_ring.next()
            eng = PELT if kc % 2 == 0 else "dve"
            a = src_fn(kc)
            kb.op(eng, TT(sq[si][:, :], a, a, ALU.mult), reads=Bsrc, writes=[Bsq[si]])
            kb.op("pe", MM(PS[pi][:, 0:TW], ONEbf, sq[si][:, :], kc == 0, kc == nch - 1, inc=True),
                  reads=[Bsq[si], Bconst], writes=[BPS[pi]])
        kb.op("act", ACTF(rstd[:, :], PS[pi][:, 0:TW], AF.Ln, bias=epst[:, 0:1], scale=1.0 / D),
              reads=[BPS[pi], Bconst], writes=[Brstd])
        kb.op("act", ACTF(rstd[:, :], rstd[:, :], AF.Exp, scale=-0.5), reads=[Brstd], writes=[Brstd])

    def make_u(t, gidx):
        c0 = t * TW
        rms_stats(lambda kc: hT[:, kc, c0:c0 + TW], [BhT[t]], KC)
        for kc in range(KC):
            kb.op("dve", STT(uT[:, kc, :], hT[:, kc, c0:c0 + TW], gn[:, gidx * 8 + kc: gidx * 8 + kc + 1],
                             rstd[:, :], ALU.mult, ALU.mult),
                  reads=[BhT[t], Brstd, Bconst], writes=[BuT])

    def post_norm_add(t, gcol_fn):
        c0 = t * TW
        rms_stats(lambda kc: yTf[:, kc * TW:(kc + 1) * TW], Byt, KC)
        for kc in range(KC):
            ti = tmp_ring.next()
            kb.op("dve", STT(tmp[ti][:, :], yTf[:, kc * TW:(kc + 1) * TW], gcol_fn(kc), rstd[:, :],
                             ALU.mult, ALU.mult), reads=Byt + [Brstd, Bconst], writes=[Btmp[ti]])
            kb.op("pool", TT(hT[:, kc, c0:c0 + TW], hT[:, kc, c0:c0 + TW], tmp[ti][:, :], ALU.add),
                  reads=[Btmp[ti], BhT[t]], writes=[BhT[t]])

    class Pipe:
        def __init__(self, depth):
            self.items = []
            self.depth = depth

        def add(self, load, compute):
            self.items.append((load, compute))

        def run(self, limit=None):
            if limit is not None:
                self.items = self.items[:limit]
            n = len(self.items)
            for i in range(min(self.depth, n)):
                if self.items[i][0]:
                    self.items[i][0]()
            for i in range(n):
                self.items[i][1]()
                j = i + self.depth
                if j < n and self.items[j][0]:
                    self.items[j][0]()

    def ffn_items(pipe, t, win, wout, gpre, gpost):
        c0 = t * TW
        pipe.add(None, lambda: make_u(t, gpre))
        wv = wview(win)
        for g in range(FC // 2):
            bufs = {}

            def load(g=g, bufs=bufs):
                ia, ib = wA_ring.next(), wA_ring.next()
                bufs["a"], bufs["b"] = ia, ib
                kb.dma("sp", wA[ia], wv[:, :, g * 256:(g + 1) * 256], reads=[Bwfull[win]], writes=[BwA[ia]])
                kb.dma("sp", wA[ib], wv[:, :, DFF + g * 256: DFF + (g + 1) * 256], reads=[Bwfull[win]],
                       writes=[BwA[ib]])

            def compute(g=g, bufs=bufs):
                ia, ib = bufs["a"], bufs["b"]
                for jj in range(2):
                    j = g * 2 + jj
                    pa, pb = main_ring.next(), main_ring.next()
                    for kc in range(KC):
                        kb.op("pe", MM(PS[pa][:, 0:TW], wA[ia][:, kc, jj * 128:(jj + 1) * 128], uT[:, kc, :],
                                       kc == 0, kc == KC - 1), reads=[BwA[ia], BuT], writes=[BPS[pa]])
                    for kc in range(KC):
                        kb.op("pe", MM(PS[pb][:, 0:TW], wA[ib][:, kc, jj * 128:(jj + 1) * 128], uT[:, kc, :],
                                       kc == 0, kc == KC - 1), reads=[BwA[ib], BuT], writes=[BPS[pb]])
                    si = sa_ring.next()
                    kb.op("act", ACTF(sa[si][:, :], PS[pa][:, 0:TW], AF.Silu), reads=[BPS[pa]], writes=[Bsa[si]])
                    kb.op("dve", TT(gT[:, j, :], sa[si][:, :], PS[pb][:, 0:TW], ALU.mult),
                          reads=[Bsa[si], BPS[pb]], writes=[BgT])
            pipe.add(load, compute)
        wo = wfull[wout].rearrange("(fc p) n -> p fc n", p=128)
        for g in range(4):
            bufs = {}

            def load(g=g, bufs=bufs):
                io = wO_ring.next()
                bufs["o"] = io
                kb.dma("sp", wO[io], wo[:, :, g * 256:(g + 1) * 256], reads=[Bwfull[wout]], writes=[BwO[io]])

            def compute(g=g, bufs=bufs):
                io = bufs["o"]
                for jj in range(2):
                    dc = g * 2 + jj
                    pi = main_ring.next()
                    for fc in range(FC):
                        kb.op("pe", MM(PS[pi][:, 0:TW], wO[io][:, fc, jj * 128:(jj + 1) * 128], gT[:, fc, :],
                                       fc == 0, fc == FC - 1), reads=[BwO[io], BgT], writes=[BPS[pi]])
                    kb.op("act", ACTF(yTf[:, dc * TW:(dc + 1) * TW], PS[pi][:, 0:TW], AF.Copy),
                          reads=[BPS[pi]], writes=Byt)
            pipe.add(load, compute)
        pipe.add(None, lambda: post_norm_add(t, lambda kc: gh[:, gpost * 8 + kc: gpost * 8 + kc + 1]))

    def proj_items(pipe, wname, col0, nchunks, consumer):
        wv = wview(wname)
        ngrp = (nchunks + 1) // 2
        for g in range(ngrp):
            bufs = {}
            ncg = min(2, nchunks - g * 2)

            def load(g=g, bufs=bufs, ncg=ncg):
                ia = wA_ring.next()
                bufs["a"] = ia
                kb.dma("sp", wA[ia][:, :, 0:ncg * 128], wv[:, :, col0 + g * 256: col0 + g * 256 + ncg * 128],
                       reads=[Bwfull[wname]], writes=[BwA[ia]])

            def compute(g=g, bufs=bufs, ncg=ncg):
                ia = bufs["a"]
                for jj in range(ncg):
                    ci = g * 2 + jj
                    pi = main_ring.next()
                    for kc in range(KC):
                        kb.op("pe", MM(PS[pi][:, 0:TW], wA[ia][:, kc, jj * 128:(jj + 1) * 128], uT[:, kc, :],
                                       kc == 0, kc == KC - 1), reads=[BwA[ia], BuT], writes=[BPS[pi]])
                    consumer(ci, pi)
            pipe.add(load, compute)

    pid_cache = {}

    def dynsrc(eng, fn):
        if "pid" not in pid_cache:
            pid_cache["pid"] = eng.partition_id()
        return fn(pid_cache["pid"])

    def dma_dyn(out, fn, reads, writes):
        idx = kb._next_dma_idx("sp")
        prev = kb.dma_val[idx]
        ex = [(("dma", idx), prev)] if prev else []
        waits = kb._deps("sp", reads, writes, ex)
        kb.dma_val[idx] = prev + 16
        tok = (("dma", idx), prev + 16)
        sem = kb.dma_sems[idx]

        def thunk(eng, waits=waits, sem=sem, out=out, fn=fn):
            for s, v in waits:
                eng.wait_ge(s, v)
            src = dynsrc(eng, fn)
            try:
                eng.dma_start(out=out, in_=src).then_inc(sem, 16)
            except Exception:
                print("DYN DMA FAIL", out, src)
                raise
        kb.prog["sp"].append(thunk)
        kb._commit(tok, reads, writes)
        return tok

    def fetch_tile(t):
        dma_dyn(L1all[:, :, :, t * TW:(t + 1) * TW],
                lambda pid, t=t: G1[t][:, :, bass.ds(pid, 1), :, :].rearrange("r k o d t -> r k (o d) t"),
                [BG1t[t]], [BL1])
        dma_dyn(LF[:, t * TW:(t + 1) * TW],
                lambda pid, t=t: GF[t][:, bass.ds(pid, 1), :].rearrange("r o t -> r (o t)"), [BGFt[t]], [BLF])

    pipeA = Pipe(2)
    for t in range(NTILE):
        ffn_items(pipeA, t, "f1in", "f1out", 0, 1)
        pipeA.add(None, (lambda t=t: make_u(t, 2)))

        def qkv_consumer(ci, pi, t=t):
            kind, hp = ci // 4, ci % 4
            si = st_ring.next()
            if kind == 0:
                kb.op("dve", TS(stage[si][:, :], PS[pi][:, 0:TW], 0.125, None, ALU.mult),
                      reads=[BPS[pi]], writes=[Bstage[si]])
            else:
                kb.op("act", ACTF(stage[si][:, :], PS[pi][:, 0:TW], AF.Copy), reads=[BPS[pi]],
                      writes=[Bstage[si]])
            dst = S1[t][kind, 2 * hp:2 * hp + 2, :, :].rearrange("h d t -> (h d) t")
            kb.dma("sp", dst, stage[si][:, :], reads=[Bstage[si]], writes=[BS1t[t]])
        proj_items(pipeA, "win", 0, 12, qkv_consumer)

        def f_load(t=t):
            kb.dma("sp", wf8[:, :, :], wview("win")[:, :, 1536:1544], reads=[Bwfull["win"]], writes=[Bwf8])

        def f_compute(t=t):
            pi = aux_ring.next()
            for kc in range(KC):
                kb.op("pe", MM(PS[pi][0:H, 0:TW], wf8[:, kc, :], uT[:, kc, :], kc == 0, kc == KC - 1),
                      reads=[Bwf8, BuT], writes=[BPS[pi]])
            kb.op("act", ACTF(lfT[:, :], PS[pi][0:H, 0:TW], AF.Exp, bias=nbf[:, 0:1], scale=-1.0),
                  reads=[BPS[pi], Bconst], writes=[BlfT])
            kb.op("act", ACTF(lfT[:, :], lfT[:, :], AF.Ln, bias=onet[0:H, 0:1], scale=1.0),
                  reads=[BlfT, Bconst], writes=[BlfT])
            kb.op("dve", TS(lfT[:, :], lfT[:, :], -1.0, None, ALU.mult), reads=[BlfT], writes=[BlfT])
            kb.dma("sp", SF[t], lfT[:, :], reads=[BlfT], writes=[BSFt[t]])
        pipeA.add(f_load, f_compute)

        def gather_tile(t=t):
            kb.collective([S1[t].rearrange("k h d t -> (k h d) t")], [G1[t].rearrange("r k h d t -> (r k h d) t")],
                          reads=[BS1t[t]], writes=[BG1t[t]])
            kb.collective([SF[t]], [GF[t].rearrange("r h t -> (r h) t")], reads=[BSFt[t]], writes=[BGFt[t]])
            if t >= 1:
                fetch_tile(t - 1)
        pipeA.add(None, gather_tile)
    if stop is not None and stop.startswith("A:"):
        pipeA.run(int(stop[2:]))
        return finish()
    pipeA.run()

    if stop == "A":
        return finish()
    fetch_tile(NTILE - 1)
    kb.barrier()

    s_ring = Ring([0, 1, 2, 3])
    pt_ring = Ring([0, 1, 2, 3])
    acc_ring = Ring([4, 5])
    o_ring = Ring([0, 1])

    kb.op("pool", MEMSET(Vaug[:, :, 64:128], 1.0), writes=[BVaug])
    for b in range(2):
        r0 = 4 * b
        kb.op("pool", MEMSET(Qaug[0:32, :], 0.0), writes=[BQaug])
        kb.op("pool", MEMSET(Kaug[0:32, :], 0.0), writes=[BKaug])
        kb.op("pool", MEMSET(Kaug[0:2, :], 1.0), writes=[BKaug])
        kb.dma("sp", Kaug[32:96, 0:NMETA], L1all[r0, 1, :, 0:NMETA], reads=[BL1], writes=[BKaug])
        for i in range(4):
            kb.dma("sp", Kaug[32:96, NMETA + i * TL: NMETA + (i + 1) * TL], L1all[r0 + i, 1, :, EX:NT],
                   reads=[BL1], writes=[BKaug])
            kb.dma("sp", Qaug[32:96, i * TL:(i + 1) * TL], L1all[r0 + i, 0, :, EX:NT], reads=[BL1], writes=[BQaug])
        kb.dma("sp", VTst[0:64, 0:NMETA], L1all[r0, 2, :, 0:NMETA], reads=[BL1], writes=[BVTst])
        pi = aux_ring.next()
        kb.op("pe", MM(PS[pi][0:NMETA, 0:64], VTst[0:64, 0:NMETA], Ibf[0:64, 0:64], True, True),
              reads=[BVTst, Bconst], writes=[BPS[pi]])
        kb.op("dve", CP(Vaug[0:NMETA, 0, 0:64], PS[pi][0:NMETA, 0:64]), reads=[BPS[pi]], writes=[BVaug])
        for i in range(4):
            kb.dma("sp", VTst[0:64, :], L1all[r0 + i, 2, :, EX:NT], reads=[BL1], writes=[BVTst])
            for g in range(2):
                pi = aux_ring.next()
                for q in range(8):
                    blk = g * 8 + q
                    kb.op("pe", MM(PS[pi][:, q * 64:(q + 1) * 64], VTst[0:64, blk * 128:(blk + 1) * 128],
                                   Ibf[0:64, 0:64], True, True), reads=[BVTst, Bconst], writes=[BPS[pi]])
                kb.op("dve", CP(Vaug[:, 1 + i * 16 + g * 8: 1 + i * 16 + g * 8 + 8, 0:64],
                                PS[pi][:, :].rearrange("p (q c) -> p q c", q=8)),
                      reads=[BPS[pi]], writes=[BVaug])
        kb.op("dve", MEMSET(lfR[:, :], 0.0), writes=[BlfR])
        kb.dma("sp", lfR[0:1, 0:NMETA], LF[r0:r0 + 1, 0:NMETA], reads=[BLF], writes=[BlfR])
        for i in range(4):
            kb.dma("sp", lfR[1 + 16 * i: 17 + 16 * i, :],
                   LF[r0 + i:r0 + i + 1, EX:NT].rearrange("o (a c) -> (o a) c", c=128), reads=[BLF], writes=[BlfR])
        pi = aux_ring.next()
        kb.op("pe", MM(PS[pi][:, 0:NBLK], lfR[:, :], I32[0:NBLK, 0:NBLK], True, True),
              reads=[BlfR, Bconst], writes=[BPS[pi]])
        kb.op("dve", CP(lfC[:, :], PS[pi][:, 0:NBLK]), reads=[BPS[pi]], writes=[BlfC])
        pi = aux_ring.next()
        kb.op("pe", MM(PS[pi][0:NBLK, 0:128], lfC[:, :], ONE32, True, True), reads=[BlfC, Bconst],
              writes=[BPS[pi]])
        kb.op("dve", CP(totT[:, :], PS[pi][0:NBLK, 0:128]), reads=[BPS[pi]], writes=[BtotT])
        pi = aux_ring.next()
        kb.op("pe", MM(PS[pi][:, 0:NBLK], U32, lfC[:, :], True, False), reads=[BlfC, Bconst], writes=[BPS[pi]])
        kb.op("pe", MM(PS[pi][:, 0:NBLK], totT[:, :], SU32[0:NBLK, 0:NBLK], False, True),
              reads=[BtotT, Bconst], writes=[BPS[pi]])
        kb.op("dve", TS(negF[:, :], PS[pi][:, 0:NBLK], -1.0, None, ALU.mult), reads=[BPS[pi]], writes=[BnegF])
        kb.op("dve", CP(Ff[:, :], PS[pi][:, 0:NBLK]), reads=[BPS[pi]], writes=[BFf])
        kb.op("dve", CP(Fp[:, :, 0], Ff[:, :]), reads=[BFf], writes=[BFp])
        kb.op("dve", TT(Fp[:, :, 1], Ff[:, :], Fp[:, :, 0], ALU.subtract), reads=[BFf, BFp], writes=[BFp])
        for g in range(16):
            pi = aux_ring.next()
            for q in range(4):
                rb = g * 4 + q
                kb.op("pe", MM(PS[pi][0:2, q * 128:(q + 1) * 128], Fp[:, rb + 1, :], Ibf, True, True),
                      reads=[BFp, Bconst], writes=[BPS[pi]])
            kb.op("dve", CP(Qaug[0:2, g * 512:(g + 1) * 512], PS[pi][0:2, :]), reads=[BPS[pi]], writes=[BQaug])

        steps = []
        for qc in range(16):
            blocks = [(-1, 0)] + [(rb, 0) for rb in range(4 * qc)] + [(4 * qc + r, r) for r in range(4)]
            for bi, (rb, r) in enumerate(blocks):
                steps.append((qc, bi, len(blocks), rb, r))
        LA = 2
        st = {}

        def s_part(i):
            qc, bi, nb, rb, r = steps[i]
            if bi == 0:
                st[("acc", qc)] = acc_ring.next()
            q0 = qc * 512
            if rb < 0:
                nk, kcol, vb = NMETA, 0, 0
            else:
                nk, kcol, vb = 128, NMETA + rb * 128, rb + 1
            diag = rb >= 4 * qc
            N = 512 - 128 * r
            si = s_ring.next()
            kb.op("pe", MM(PS[si][0:nk, 0:N], Kaug[0:96, kcol:kcol + nk], Qaug[0:96, q0 + r * 128: q0 + 512],
                           True, not diag), reads=[BKaug, BQaug], writes=[BPS[si]])
            if diag:
                kb.op("pe", MM(PS[si][:, 0:128], Ibf, TRIbf, False, True), reads=[Bconst], writes=[BPS[si]])
            pti = pt_ring.next()
            kb.op("act", ACTF(PT[pti][0:nk, 0:N], PS[si][0:nk, 0:N], AF.Exp, bias=negF[0:nk, vb:vb + 1]),
                  reads=[BPS[si], BnegF], writes=[BPT[pti]])
            st[i] = (pti, nk, vb, N)

        def pv_part(i):
            qc, bi, nb, rb, r = steps[i]
            pti, nk, vb, N = st.pop(i)
            ai = st[("acc", qc)]
            kb.op("pe", MM(PS[ai][:, r * 128:512], Vaug[0:nk, vb, :], PT[pti][0:nk, 0:N],
                           bi == 0, bi == nb - 1, inc=True), reads=[BVaug, BPT[pti]], writes=[BPS[ai]])
            if bi == nb - 1:
                oi = o_ring.next()
                kb.op("dve", CP(osb[oi][:, :], PS[ai][:, :]), reads=[BPS[ai]], writes=[Bosb[oi]])
                kb.dma("sp", den[oi][:, :], osb[oi][64:128, :], reads=[Bosb[oi]], writes=[Bden[oi]])
                kb.op("dve", RECIP(den[oi][:, :], den[oi][:, :]), reads=[Bden[oi]], writes=[Bden[oi]])
                kb.op("dve", TT(abf[oi][:, :], osb[oi][0:64, :], den[oi][:, :], ALU.mult),
                      reads=[Bosb[oi], Bden[oi]], writes=[Babf[oi]])
                tcid = 4 * b + qc // 4
                kb.dma("sp", S2[tcid, :, (qc % 4) * 512:(qc % 4 + 1) * 512], abf[oi][:, :], reads=[Babf[oi]],
                       writes=[BS2])

        nst = len(steps)
        for i in range(nst + LA):
            if i < nst:
                s_part(i)
            if i - LA >= 0:
                pv_part(i - LA)

    if stop == "B":
        return finish()
    kb.collective([S2.rearrange("c d t -> (c d) t")], [G2.rearrange("h c d t -> (h c d) t")],
                  reads=[BS2], writes=[BG2])
    kb.barrier()

    dma_dyn(L2, lambda pid: G2[:, bass.ds(pid, 1), :, :].rearrange("h o d t -> h (o d) t"), [BG2], [BL2])
    kb.op("pool", MEMSET(ccbuf[:, :, 0:2], 0.0), writes=[Bcc])
    pipeC = Pipe(2)
    yv = yTf[:, :].rearrange("p (k c) -> p k c", k=KC)
    for t in range(NTILE):
        c0 = t * TW
        pipeC.add(None, (lambda t=t: make_u(t, 2)))

        def attn_load(t=t):
            a0 = EX if t == 0 else 0
            tl0 = t * TW + a0 - EX
            if t == 0:
                kb.op("pool", MEMSET(attnT[:, :, 0:EX], 0.0), writes=[BattnT])
            for wi, wname in enumerate(("wab", "wcb")):
                kb.dma("sp", wBr[wi], wfull[wname].rearrange("(kc p) n -> p kc n", p=128),
                       reads=[Bwfull[wname]], writes=[BwO[wi]])
            for h in range(H):
                kb.dma("sp", attnT[(h % 2) * 64:(h % 2) * 64 + 64, h // 2, a0:TW], L2[h, :, tl0:tl0 + TW - a0],
                       reads=[BL2], writes=[BattnT])

        def conv_consumer(ci, pi, t=t):
            grp, ch = ci // 4, ci % 4
            if grp == 0:
                kb.op("act", ACTF(cbT[:, ch, :], PS[pi][:, 0:TW], AF.Copy), reads=[BPS[pi]], writes=[BcbT])
            elif grp == 1:
                kb.op("act", ACTF(ccbuf[:, ch, 2:2 + TW], PS[pi][:, 0:TW], AF.Copy), reads=[BPS[pi]],
                      writes=[Bcc])
            else:
                kb.op("dve", TT(ccbuf[:, ch, 2:2 + TW], ccbuf[:, ch, 2:2 + TW], PS[pi][:, 0:TW], ALU.mult),
                      reads=[BPS[pi], Bcc], writes=[Bcc])
                ti = tmp_ring.next()
                kb.op("dve", TS(tmp[ti][:, :], ccbuf[:, ch, 0:TW], cw[:, 0 * 4 + ch: 0 * 4 + ch + 1], None, ALU.mult),
                      reads=[Bcc, Bconst], writes=[Btmp[ti]])
                kb.op("dve", STT(tmp[ti][:, :], ccbuf[:, ch, 1:1 + TW], cw[:, 1 * 4 + ch: 1 * 4 + ch + 1],
                                 tmp[ti][:, :], ALU.mult, ALU.add), reads=[Bcc, Bconst, Btmp[ti]], writes=[Btmp[ti]])
                kb.op("dve", STT(tmp[ti][:, :], ccbuf[:, ch, 2:2 + TW], cw[:, 2 * 4 + ch: 2 * 4 + ch + 1],
                                 tmp[ti][:, :], ALU.mult, ALU.add), reads=[Bcc, Bconst, Btmp[ti]], writes=[Btmp[ti]])
                kb.op("dve", TT(convg[:, ch, :], tmp[ti][:, :], cbT[:, ch, :], ALU.mult),
                      reads=[Btmp[ti], BcbT], writes=[Bconvg])
                kb.op("dve", CP(ccbuf[:, ch, 0:2], ccbuf[:, ch, TW:TW + 2]), reads=[Bcc], writes=[Bcc])
        pipeC.add(attn_load, lambda: None)
        proj_items(pipeC, "win", 1544, 12, conv_consumer)

        def gate_consumer(ci, pi, t=t):
            which, dc = ci // 8, ci % 8
            src = attnT if which == 0 else convg
            Bsrc = BattnT if which == 0 else Bconvg
            gi = sig_ring.next()
            kb.op("act", ACTF(sig[gi][:, :], PS[pi][:, 0:TW], AF.Sigmoid), reads=[BPS[pi]], writes=[Bsig[gi]])
            p2 = main_ring.next()
            for kc in range(4):
                kb.op("pe", MM(PS[p2][:, 0:TW], wBr[which][:, kc, dc * 128:(dc + 1) * 128], src[:, kc, :],
                               kc == 0, kc == 3), reads=[BwO[which], Bsrc], writes=[BPS[p2]])
            if which == 0:
                kb.op("dve", TT(yv[:, dc, :], sig[gi][:, :], PS[p2][:, 0:TW], ALU.mult),
                      reads=[Bsig[gi], BPS[p2]], writes=Byt)
            else:
                ti = tmp_ring.next()
                kb.op("dve", TT(tmp[ti][:, :], sig[gi][:, :], PS[p2][:, 0:TW], ALU.mult),
                      reads=[Bsig[gi], BPS[p2]], writes=[Btmp[ti]])
                kb.op("dve", TT(gT[:, dc, :], tmp[ti][:, :], yv[:, dc, :], ALU.add),
                      reads=[Btmp[ti]] + Byt, writes=[BgT])
        proj_items(pipeC, "win", 3080, 16, gate_consumer)

        def wout_items(t=t):
            wv = wview("wout")
            for g in range(4):
                bufs = {}

                def load(g=g, bufs=bufs):
                    ia = wA_ring.next()
                    bufs["a"] = ia
                    kb.dma("sp", wA[ia], wv[:, :, g * 256:(g + 1) * 256], reads=[Bwfull["wout"]], writes=[BwA[ia]])

                def compute(g=g, bufs=bufs):
                    ia = bufs["a"]
                    for jj in range(2):
                        dc = g * 2 + jj
                        pi = main_ring.next()
                        for kc in range(KC):
                            kb.op("pe", MM(PS[pi][:, 0:TW], wA[ia][:, kc, jj * 128:(jj + 1) * 128], gT[:, kc, :],
                                           kc == 0, kc == KC - 1), reads=[BwA[ia], BgT], writes=[BPS[pi]])
                        kb.op("act", ACTF(yv[:, dc, :], PS[pi][:, 0:TW], AF.Copy), reads=[BPS[pi]], writes=Byt)
                pipeC.add(load, compute)
        wout_items()
        pipeC.add(None, (lambda t=t: post_norm_add(t, lambda kc: gn[:, 3 * 8 + kc: 3 * 8 + kc + 1])))
        ffn_items(pipeC, t, "f2in", "f2out", 4, 5)

        def out_store(t=t):
            c0 = t * TW
            a0 = EX if t == 0 else 0
            col = c0 + a0
            while col < c0 + TW:
                n = min(128, c0 + TW - col)
                xi = (col // 128) % 2
                xb, Bx = xs[xi], Byt[xi]
                for half in range(2):
                    pi = main_ring.next()
                    for q in range(4):
                        kc = half * 4 + q
                        kb.op("pe", MM(PS[pi][0:n, q * 128:(q + 1) * 128], hT[:, kc, col:col + n], I32,
                                       True, True), reads=[BhT[t], Bconst], writes=[BPS[pi]])
                    eng = cpe.next()
                    if eng == "dve":
                        kb.op("dve", CP(xb[0:n, half * 512:(half + 1) * 512], PS[pi][0:n, :]), reads=[BPS[pi]],
                              writes=[Bx])
                    else:
                        kb.op("act", ACTF(xb[0:n, half * 512:(half + 1) * 512], PS[pi][0:n, :], AF.Copy),
                              reads=[BPS[pi]], writes=[Bx])
                kb.dma("sp", y_out[col - EX: col - EX + n, :], xb[0:n, :], reads=[Bx])
                col += n
        pipeC.add(None, out_store)
    if stop is not None and stop.startswith("C:"):
        pipeC.run(int(stop[2:]))
        return finish()
    pipeC.run()

    toks = kb.all_tokens()
    waits = kb._deps("sp", (), (), toks)

    def fin(eng, waits=waits):
        for s, v in waits:
            eng.wait_ge(s, v)
    kb.prog["sp"].append(fin)
    kb.emit()
    return nc


PELT = "dve"
_NC_CACHE = {}
_STOP = None


def _consts():
    c = np.zeros((128, 640), np.float32)
    i = np.arange(128)
    c[:, 0:128] = np.eye(128, dtype=np.float32)
    c[:, 128:256] = (i[:, None] <= i[None, :]).astype(np.float32)
    c[:, 256:384] = (i[:, None] < i[None, :]).astype(np.float32)
    c[:, 384:512] = 1.0
    c[:, 512:640] = (i[:, None] > i[None, :]).astype(np.float32) * -30000.0
    return c


def kernel(x, meta_tokens, w_in, b_forget, conv_w, w_attn_branch, w_conv_branch, w_out,
           g_ffn1_pre, g_ffn1_post, w_ffn1_in, w_ffn1_out,
           g_mix_pre, g_mix_post, g_ffn2_pre, g_ffn2_post, w_ffn2_in, w_ffn2_out):
    f = lambda a: np.ascontiguousarray(np.asarray(a, dtype=np.float32))
    x = f(x)
    meta = f(meta_tokens)
    ws = {"f1in": f(w_ffn1_in)[0], "f1out": f(w_ffn1_out)[0], "win": f(w_in)[0], "wab": f(w_attn_branch)[0],
          "wcb": f(w_conv_branch)[0], "wout": f(w_out)[0], "f2in": f(w_ffn2_in)[0], "f2out": f(w_ffn2_out)[0]}
    gains = [f(g)[0] for g in (g_ffn1_pre, g_ffn1_post, g_mix_pre, g_mix_post, g_ffn2_pre, g_ffn2_post)]
    gn = np.zeros((128, 48), np.float32)
    for i, g in enumerate(gains):
        gn[:, i * 8:(i + 1) * 8] = g.reshape(8, 128).T
    cwm = np.zeros((128, 12), np.float32)
    cwv = f(conv_w)[0]
    for j in range(3):
        cwm[:, j * 4:(j + 1) * 4] = cwv[j].reshape(4, 128).T
    bfg = f(b_forget)[0].reshape(8, 1)
    cst = _consts()

    if "nc" not in _NC_CACHE:
        _NC_CACHE["nc"] = build_nc(_STOP)
    nc = _NC_CACHE["nc"]

    in_maps = []
    for c in range(NCORES):
        b, j = c // 4, c % 4
        xe = np.zeros((EX, D), np.float32)
        xe[0:NMETA] = meta
        xe[EX - 2:EX] = meta[NMETA - 2:NMETA] if j == 0 else x[b, j * TL - 2: j * TL]
        m = {"x": np.ascontiguousarray(x[b, j * TL:(j + 1) * TL]), "xe": xe, "gn": gn, "cw": cwm,
             "bfg": bfg, "cst": cst}
        for name, K, N in WSPEC:
            rows = K // NCORES
            m["w_" + name] = np.ascontiguousarray(ws[name][c * rows:(c + 1) * rows])
        in_maps.append(m)
    res = run_bass_kernel_spmd(nc, in_maps, core_ids=list(range(NCORES)))
    out = np.zeros((2, SEQ, D), np.float32)
    for c in range(NCORES):
        b, j = c // 4, c % 4
        out[b, j * TL:(j + 1) * TL] = res.results[c]["y"]
    return out
```

```python
import numpy as np
import concourse.bass as bass
import concourse.mybir as mybir
from concourse.bass_utils import run_bass_kernel_spmd

F32 = mybir.dt.float32
BF16 = mybir.dt.bfloat16
AF = mybir.ActivationFunctionType
ALU = mybir.AluOpType

NCORES = 8
D = 1024
KC = 8
SEQ = 8192
NMETA = 16
TL = 2048
EX = 32
NT = TL + EX
TW = 416
NTILE = NT // TW
DFF = 2816
FC = DFF // 128
INC = 5128
H = 8
DH = 64
EPS = 1e-6
NBLK = 65
NDMA_SEMS = 28


class Buf:
    __slots__ = ("name", "last_w", "reads")

    def __init__(self, name):
        self.name = name
        self.last_w = None
        self.reads = []


class KB:
    ENGS = ("pe", "act", "dve", "pool", "sp")

    def __init__(self, nc):
        self.nc = nc
        self.sem = {e: nc.alloc_semaphore("sem_" + e) for e in self.ENGS}
        self.cnt = {e: 0 for e in self.ENGS}
        self.waited = {e: {} for e in self.ENGS}
        self.prog = {e: [] for e in self.ENGS}
        self.dma_sems = [nc.alloc_semaphore("dsem%d" % i) for i in range(NDMA_SEMS)]
        self.dma_val = [0] * NDMA_SEMS
        self.dma_next = 0
        self.cc_sem = nc.alloc_semaphore("ccsem")
        self.cc_val = 0

    def _semh(self, key):
        if key[0] == "eng":
            return self.sem[key[1]]
        if key[0] == "dma":
            return self.dma_sems[key[1]]
        return self.cc_sem

    def _deps(self, engine, reads, writes, extra=()):
        deps = {}

        def add(tok):
            if tok is None:
                return
            k, v = tok
            if deps.get(k, 0) < v:
                deps[k] = v
        for r in reads:
            add(r.last_w)
        for w in writes:
            add(w.last_w)
            for t in w.reads:
                add(t)
        for t in extra:
            add(t)
        waits = []
        for k, v in deps.items():
            if k == ("eng", "pe") and engine == "pe":
                continue
            if self.waited[engine].get(k, 0) >= v:
                continue
            self.waited[engine][k] = v
            waits.append((self._semh(k), v))
        return waits

    def _commit(self, tok, reads, writes):
        for r in reads:
            r.reads.append(tok)
            if len(r.reads) > 64:
                best = {}
                for k, v in r.reads:
                    if best.get(k, 0) < v:
                        best[k] = v
                r.reads = list(best.items())
        for w in writes:
            w.last_w = tok
            w.reads = []

    def op(self, engine, fn, reads=(), writes=(), extra=(), inc=True):
        if isinstance(fn, MMop):
            inc = fn.stop
        waits = self._deps(engine, reads, writes, extra)
        if inc:
            self.cnt[engine] += 1
            tok = (("eng", engine), self.cnt[engine])
        else:
            tok = (("eng", engine), self.cnt[engine] + 1)
        sem = self.sem[engine]

        def thunk(eng, waits=waits, fn=fn, sem=sem, inc=inc):
            for s, v in waits[:-1]:
                eng.wait_ge(s, v)
            ins = fn(eng)
            if waits:
                ins._wait_ge(waits[-1][0], waits[-1][1])
            if inc:
                ins.then_inc(sem, 1)
        self.prog[engine].append(thunk)
        self._commit(tok, reads, writes)
        return tok

    def _next_dma_idx(self, engine):
        if engine == "pool":
            self.pool_next = (getattr(self, "pool_next", -1) + 1) % 4
            return NDMA_SEMS - 4 + self.pool_next
        idx = self.dma_next
        self.dma_next = (self.dma_next + 1) % (NDMA_SEMS - 4)
        return idx

    def dma(self, engine, out, in_, reads=(), writes=(), extra=()):
        idx = self._next_dma_idx(engine)
        prev = self.dma_val[idx]
        ex = list(extra)
        if prev:
            ex.append((("dma", idx), prev))
        waits = self._deps(engine, reads, writes, ex)
        self.dma_val[idx] = prev + 16
        tok = (("dma", idx), prev + 16)
        sem = self.dma_sems[idx]

        def thunk(eng, waits=waits, sem=sem, out=out, in_=in_):
            for s, v in waits:
                eng.wait_ge(s, v)
            eng.dma_start(out=out, in_=in_).then_inc(sem, 16)
        self.prog[engine].append(thunk)
        self._commit(tok, reads, writes)
        return tok

    def collective(self, ins, outs, reads=(), writes=()):
        waits = self._deps("pool", reads, writes)
        self.cc_val += 1
        tok = (("cc", 0), self.cc_val)
        sem = self.cc_sem
        groups = [list(range(NCORES))]

        def thunk(eng, waits=waits):
            for s, v in waits:
                eng.wait_ge(s, v)
            eng.collective_compute("AllGather", ALU.bypass, replica_groups=groups,
                                   ins=ins, outs=outs).then_inc(sem)
        self.prog["pool"].append(thunk)
        self._commit(tok, reads, writes)
        return tok

    def all_tokens(self):
        toks = [(("eng", e), self.cnt[e]) for e in self.ENGS if self.cnt[e]]
        toks += [(("dma", i), self.dma_val[i]) for i in range(NDMA_SEMS) if self.dma_val[i]]
        if self.cc_val:
            toks.append((("cc", 0), self.cc_val))
        return toks

    def barrier(self):
        toks = self.all_tokens()
        for e in self.ENGS:
            waits = self._deps(e, (), (), [t for t in toks if t[0] != ("eng", e)])

            def thunk(eng, waits=waits):
                for s, v in waits:
                    eng.wait_ge(s, v)
            self.prog[e].append(thunk)

    def emit(self):
        nc = self.nc
        with nc.Block() as block:
            @block.tensor
            def _(e):
                for t in self.prog["pe"]:
                    t(e)

            @block.scalar
            def _(e):
                for t in self.prog["act"]:
                    t(e)

            @block.vector
            def _(e):
                for t in self.prog["dve"]:
                    t(e)

            @block.gpsimd
            def _(e):
                for t in self.prog["pool"]:
                    t(e)

            @block.sync
            def _(e):
                for t in self.prog["sp"]:
                    t(e)


class MMop:
    def __init__(self, out, lhsT, rhs, start, stop, inc=None):
        self.a = (out, lhsT, rhs, start, stop)
        self.stop = stop if inc is None else inc

    def __call__(self, e):
        out, lhsT, rhs, start, stop = self.a
        return e.matmul(out, lhsT=lhsT, rhs=rhs, start=start, stop=stop, skip_group_check=True)


def MM(out, lhsT, rhs, start, stop, inc=None):
    return MMop(out, lhsT, rhs, start, stop, inc)


def ACTF(out, in_, func, bias=None, scale=1.0):
    if bias is None:
        return lambda e: e.activation(out=out, in_=in_, func=func, scale=scale)
    return lambda e: e.activation(out=out, in_=in_, func=func, bias=bias, scale=scale)


def CP(out, in_):
    return lambda e: e.tensor_copy(out=out, in_=in_)


def TT(out, in0, in1, op):
    return lambda e: e.tensor_tensor(out=out, in0=in0, in1=in1, op=op)


def TS(out, in0, s1, s2, op0, op1=None):
    if op1 is None:
        return lambda e: e.tensor_scalar(out=out, in0=in0, scalar1=s1, scalar2=None, op0=op0)
    return lambda e: e.tensor_scalar(out=out, in0=in0, scalar1=s1, scalar2=s2, op0=op0, op1=op1)


def STT(out, in0, scalar, in1, op0, op1):
    return lambda e: e.scalar_tensor_tensor(out=out, in0=in0, scalar=scalar, in1=in1, op0=op0, op1=op1)


def MEMSET(ap, val):
    return lambda e: e.memset(ap, val)


def RECIP(out, in_):
    return lambda e: e.reciprocal(out=out, in_=in_)


class Ring:
    def __init__(self, items):
        self.items = items
        self.i = 0

    def next(self):
        it = self.items[self.i % len(self.items)]
        self.i += 1
        return it


WSPEC = [
    ("f1in", D, 2 * DFF), ("f1out", DFF, D), ("win", D, INC), ("wab", 512, D), ("wcb", 512, D),
    ("wout", D, D), ("f2in", D, 2 * DFF), ("f2out", DFF, D),
]


def build_nc(stop=None):
    nc = bass.Bass("TRN2", target_bir_lowering=False)
    nc.allow_low_precision("bf16 matmul operands with fp32 PSUM accumulation")
    kb = KB(nc)

    x_in = nc.dram_tensor("x", [TL, D], F32, kind="ExternalInput").ap()
    xe_in = nc.dram_tensor("xe", [EX, D], F32, kind="ExternalInput").ap()
    gn_in = nc.dram_tensor("gn", [128, 48], F32, kind="ExternalInput").ap()
    cw_in = nc.dram_tensor("cw", [128, 12], F32, kind="ExternalInput").ap()
    bfg_in = nc.dram_tensor("bfg", [H, 1], F32, kind="ExternalInput").ap()
    cst_in = nc.dram_tensor("cst", [128, 640], F32, kind="ExternalInput").ap()
    y_out = nc.dram_tensor("y", [TL, D], F32, kind="ExternalOutput").ap()
    wsh, wsend, wfull, Bwsend, Bwfull = {}, {}, {}, {}, {}
    for name, K, N in WSPEC:
        wsh[name] = nc.dram_tensor("w_" + name, [K // NCORES, N], F32, kind="ExternalInput").ap()
        wsend[name] = nc.dram_tensor("ws_" + name, [K // NCORES, N], BF16).ap()
        wfull[name] = nc.dram_tensor("wf_" + name, [K, N], BF16).ap()
        Bwsend[name] = Buf("ws_" + name)
        Bwfull[name] = Buf("wf_" + name)
    S1 = [nc.dram_tensor("S1_%d" % t, [3, H, DH, TW], BF16).ap() for t in range(NTILE)]
    G1 = [nc.dram_tensor("G1_%d" % t, [NCORES, 3, H, DH, TW], BF16).ap() for t in range(NTILE)]
    SF = [nc.dram_tensor("SF_%d" % t, [H, TW], F32).ap() for t in range(NTILE)]
    GF = [nc.dram_tensor("GF_%d" % t, [NCORES, H, TW], F32).ap() for t in range(NTILE)]
    BS1t = [Buf("S1_%d" % t) for t in range(NTILE)]
    BG1t = [Buf("G1_%d" % t) for t in range(NTILE)]
    BSFt = [Buf("SF_%d" % t) for t in range(NTILE)]
    BGFt = [Buf("GF_%d" % t) for t in range(NTILE)]
    S2 = nc.dram_tensor("S2", [NCORES, DH, TL], BF16).ap()
    G2 = nc.dram_tensor("G2", [H, NCORES, DH, TL], BF16).ap()
    BS1, BG1, BSF, BGF, BS2, BG2 = [Buf(n) for n in ("S1", "G1", "SF", "GF", "S2", "G2")]
    L1all = nc.dram_tensor("L1all", [NCORES, 3, DH, NT], BF16).ap()
    LF = nc.dram_tensor("LF", [NCORES, NT], F32).ap()
    L2 = nc.dram_tensor("L2", [H, DH, TL], BF16).ap()
    BL1, BLF, BL2 = [Buf(n) for n in ("L1", "LF", "L2")]

    PS = [nc.alloc_psum_tensor("ps%d" % i, [128, 512], F32) for i in range(8)]
    BPS = [Buf("ps%d" % i) for i in range(8)]

    def sb(name, shape, dt):
        return nc.alloc_sbuf_tensor("sb_" + name, shape, dt)

    hT = sb("hT", [128, KC, NT], F32)
    BhT = [Buf("hT%d" % t) for t in range(NTILE)]
    AR = 28800
    arena = sb("arena", [128, AR], BF16)
    uT = sb("uT", [128, KC, TW], BF16)
    BuT = Buf("uT")
    yTf = sb("yT", [128, KC * TW], F32)
    Byt = [Buf("yt%d" % i) for i in range(3)]
    sa = [sb("sa%d" % i, [128, TW], F32) for i in range(2)]
    Bsa = [Buf("sa%d" % i) for i in range(2)]
    tmp = [sb("tmp%d" % i, [128, TW], F32) for i in range(2)]
    Btmp = [Buf("tmp%d" % i) for i in range(2)]
    rstd = sb("rstd", [128, TW], F32)
    Brstd = Buf("rstd")
    sq = [sb("sq%d" % i, [128, TW], BF16) for i in range(2)]
    Bsq = [Buf("sq%d" % i) for i in range(2)]
    stage = [sb("stage%d" % i, [128, TW], BF16) for i in range(3)]
    Bstage = [Buf("stage%d" % i) for i in range(3)]
    lfT = sb("lfT", [H, TW], F32)
    BlfT = Buf("lfT")
    wf8 = sb("wf8", [128, KC, 8], BF16)
    Bwf8 = Buf("wf8")
    gn = sb("gn", [128, 48], F32)
    gh = sb("gh", [128, 48], F32)
    cw = sb("cw", [128, 12], F32)
    bfg = sb("bfg", [H, 1], F32)
    nbf = sb("nbf", [H, 1], F32)
    cst = sb("cst", [128, 640], F32)
    cbf = sb("cbf", [128, 384], BF16)
    epst = sb("epst", [128, 1], F32)
    onet = sb("onet", [128, 1], F32)
    Bconst = Buf("const")
    ccbuf = sb("ccbuf", [128, 4, TW + 2], F32)
    Bcc = Buf("cc")
    cbT = sb("cbT", [128, 4, TW], F32)
    BcbT = Buf("cbT")
    convg = sb("convg", [128, 4, TW], BF16)
    Bconvg = Buf("convg")
    attnT = sb("attnT", [128, 4, TW], BF16)
    BattnT = Buf("attnT")
    sig = [sb("sig%d" % i, [128, TW], F32) for i in range(2)]
    Bsig = [Buf("sig%d" % i) for i in range(2)]
    PT = [sb("PT%d" % i, [128, 512], BF16) for i in range(4)]
    BPT = [Buf("PT%d" % i) for i in range(4)]
    osb = [sb("osb%d" % i, [128, 512], F32) for i in range(2)]
    Bosb = [Buf("osb%d" % i) for i in range(2)]
    den = [sb("den%d" % i, [64, 512], F32) for i in range(2)]
    Bden = [Buf("den%d" % i) for i in range(2)]
    abf = [sb("abf%d" % i, [64, 512], BF16) for i in range(2)]
    Babf = [Buf("abf%d" % i) for i in range(2)]
    lfR = sb("lfR", [NBLK, 128], F32)
    lfC = sb("lfC", [128, NBLK], F32)
    totT = sb("totT", [NBLK, 128], F32)
    negF = sb("negF", [128, NBLK], F32)
    Ff = sb("Ff", [128, NBLK], F32)
    Fp = sb("Fp", [128, NBLK, 2], BF16)
    BlfR, BlfC, BtotT, BnegF, BFf, BFp = [Buf(n) for n in ("lfR", "lfC", "totT", "negF", "Ff", "Fp")]

    def v3(off, k, c):
        return arena[:, off:off + k * c].rearrange("p (k c) -> p k c", k=k)

    gT = v3(0, FC, TW)
    BgT = Buf("gT")
    off = FC * TW
    wA = [v3(off + i * 2048, KC, 256) for i in range(4)]
    BwA = [Buf("wA%d" % i) for i in range(4)]
    off += 4 * 2048
    wO = [v3(off + i * 5632, FC, 256) for i in range(2)]
    BwO = [Buf("wO%d" % i) for i in range(2)]
    wBr = [v3(off + i * 5632, 4, 1024) for i in range(2)]
    off += 2 * 5632
    assert off <= AR
    Qaug = arena[:, 0:SEQ]
    Kaug = arena[:, SEQ:SEQ + SEQ + NMETA]
    o2 = 2 * SEQ + NMETA
    Vaug = v3(o2, NBLK, 128)
    o2 += NBLK * 128
    VTst = arena[:, o2:o2 + TL]
    assert o2 + TL <= AR
    BQaug, BKaug, BVaug, BVTst = [Buf(n) for n in ("Qaug", "Kaug", "Vaug", "VTst")]

    I32 = cst[:, 0:128]
    U32 = cst[:, 128:256]
    SU32 = cst[:, 256:384]
    ONE32 = cst[:, 384:512]
    Ibf = cbf[:, 0:128]
    TRIbf = cbf[:, 128:256]
    ONEbf = cbf[:, 256:384]

    def finish():
        toks = kb.all_tokens()
        waits = kb._deps("sp", (), (), toks)

        def fin(eng, waits=waits):
            for s_, v in waits:
                eng.wait_ge(s_, v)
        kb.prog["sp"].append(fin)
        kb.emit()
        return nc

    kb.dma("sp", cst[:, :], cst_in, writes=[Bconst])
    kb.dma("sp", gn[:, :], gn_in, writes=[Bconst])
    kb.dma("sp", cw[:, :], cw_in, writes=[Bconst])
    kb.dma("sp", bfg[:, :], bfg_in, writes=[Bconst])
    kb.op("dve", CP(cbf[:, 0:128], cst[:, 0:128]), reads=[Bconst], writes=[Bconst])
    kb.op("dve", CP(cbf[:, 128:256], cst[:, 512:640]), reads=[Bconst], writes=[Bconst])
    kb.op("dve", CP(cbf[:, 256:384], cst[:, 384:512]), reads=[Bconst], writes=[Bconst])
    kb.op("dve", TS(gh[:, :], gn[:, :], 0.5, None, ALU.mult), reads=[Bconst], writes=[Bconst])
    kb.op("dve", TS(nbf[:, :], bfg[:, :], -1.0, None, ALU.mult), reads=[Bconst], writes=[Bconst])
    kb.op("dve", MEMSET(epst[:, :], EPS), writes=[Bconst])
    kb.op("dve", MEMSET(onet[:, :], 1.0), writes=[Bconst])

    for name, K, N in WSPEC:
        kb.dma("pool", wsend[name], wsh[name], writes=[Bwsend[name]])
        kb.collective([wsend[name]], [wfull[name]], reads=[Bwsend[name]], writes=[Bwfull[name]])

    xs = [yTf[:, 0:1024], yTf[:, 1024:2048]]
    psr = Ring([4, 5, 6])
    cpe = Ring(["dve", "act"])
    nblk_in = 1 + TL // 128
    for bi in range(nblk_in):
        rows = EX if bi == 0 else 128
        c0 = 0 if bi == 0 else EX + (bi - 1) * 128
        src = xe_in if bi == 0 else x_in[(bi - 1) * 128: bi * 128, :]
        xb = xs[bi % 2]
        Bx = Byt[bi % 2]
        kb.dma("sp", xb[0:rows, :], src, writes=[Bx])
        tt = c0 // TW
        tts = sorted(set([c0 // TW, (c0 + rows - 1) // TW]))
        for half in range(2):
            pi = psr.next()
            for q in range(4):
                kc = half * 4 + q
                kb.op("pe", MM(PS[pi][:, q * 128: q * 128 + rows], xb[0:rows, kc * 128:(kc + 1) * 128],
                               I32[0:rows, 0:rows], True, True), reads=[Bx, Bconst], writes=[BPS[pi]])
            eng = cpe.next()
            outv = hT[:, half * 4:(half + 1) * 4, c0:c0 + rows]
            inv = PS[pi][:, :].rearrange("p (q c) -> p q c", q=4)[:, :, 0:rows]
            if eng == "dve":
                kb.op("dve", CP(outv, inv), reads=[BPS[pi]], writes=[BhT[t] for t in tts])
            else:
                kb.op("act", ACTF(outv, inv, AF.Copy), reads=[BPS[pi]], writes=[BhT[t] for t in tts])

    if stop == "1":
        return finish()
    main_ring = Ring([0, 1, 2, 3, 4, 5])
    aux_ring = Ring([6, 7])
    wA_ring = Ring([0, 1, 2, 3])
    wO_ring = Ring([0, 1])
    sa_ring = Ring([0, 1])
    tmp_ring = Ring([0, 1])
    sq_ring = Ring([0, 1])
    st_ring = Ring([0, 1, 2])
    sig_ring = Ring([0, 1])

    def wview(name):
        return wfull[name].rearrange("(kc p) n -> p kc n", p=128)

    def rms_stats(src_fn, Bsrc, nch):
        pi = aux_ring.next()
        for kc in range(nch):
            si = sq_ring.next()
            eng = PELT if kc % 2 == 0 else "dve"
            a = src_fn(kc)
            kb.op(eng, TT(sq[si][:, :], a, a, ALU.mult), reads=Bsrc, writes=[Bsq[si]])
            kb.op("pe", MM(PS[pi][:, 0:TW], ONEbf, sq[si][:, :], kc == 0, kc == nch - 1, inc=True),
                  reads=[Bsq[si], Bconst], writes=[BPS[pi]])
        kb.op("act", ACTF(rstd[:, :], PS[pi][:, 0:TW], AF.Ln, bias=epst[:, 0:1], scale=1.0 / D),
              reads=[BPS[pi], Bconst], writes=[Brstd])
        kb.op("act", ACTF(rstd[:, :], rstd[:, :], AF.Exp, scale=-0.5), reads=[Brstd], writes=[Brstd])

    def make_u(t, gidx):
        c0 = t * TW
        rms_stats(lambda kc: hT[:, kc, c0:c0 + TW], [BhT[t]], KC)
        for kc in range(KC):
            kb.op("dve", STT(uT[:, kc, :], hT[:, kc, c0:c0 + TW], gn[:, gidx * 8 + kc: gidx * 8 + kc + 1],
                             rstd[:, :], ALU.mult, ALU.mult),
                  reads=[BhT[t], Brstd, Bconst], writes=[BuT])

    def post_norm_add(t, gcol_fn):
        c0 = t * TW
        rms_stats(lambda kc: yTf[:, kc * TW:(kc + 1) * TW], Byt, KC)
        for kc in range(KC):
            ti = tmp_ring.next()
            kb.op("dve", STT(tmp[ti][:, :], yTf[:, kc * TW:(kc + 1) * TW], gcol_fn(kc), rstd[:, :],
                             ALU.mult, ALU.mult), reads=Byt + [Brstd, Bconst], writes=[Btmp[ti]])
            kb.op("pool", TT(hT[:, kc, c0:c0 + TW], hT[:, kc, c0:c0 + TW], tmp[ti][:, :], ALU.add),
                  reads=[Btmp[ti], BhT[t]], writes=[BhT[t]])

    class Pipe:
        def __init__(self, depth):
            self.items = []
            self.depth = depth

        def add(self, load, compute):
            self.items.append((load, compute))

        def run(self, limit=None):
            if limit is not None:
                self.items = self.items[:limit]
            n = len(self.items)
            for i in range(min(self.depth, n)):
                if self.items[i][0]:
                    self.items[i][0]()
            for i in range(n):
                self.items[i][1]()
                j = i + self.depth
                if j < n and self.items[j][0]:
                    self.items[j][0]()

    def ffn_items(pipe, t, win, wout, gpre, gpost):
        c0 = t * TW
        pipe.add(None, lambda: make_u(t, gpre))
        wv = wview(win)
        for g in range(FC // 2):
            bufs = {}

            def load(g=g, bufs=bufs):
                ia, ib = wA_ring.next(), wA_ring.next()
                bufs["a"], bufs["b"] = ia, ib
                kb.dma("sp", wA[ia], wv[:, :, g * 256:(g + 1) * 256], reads=[Bwfull[win]], writes=[BwA[ia]])
                kb.dma("sp", wA[ib], wv[:, :, DFF + g * 256: DFF + (g + 1) * 256], reads=[Bwfull[win]],
                       writes=[BwA[ib]])

            def compute(g=g, bufs=bufs):
                ia, ib = bufs["a"], bufs["b"]
                for jj in range(2):
                    j = g * 2 + jj
                    pa, pb = main_ring.next(), main_ring.next()
                    for kc in range(KC):
                        kb.op("pe", MM(PS[pa][:, 0:TW], wA[ia][:, kc, jj * 128:(jj + 1) * 128], uT[:, kc, :],
                                       kc == 0, kc == KC - 1), reads=[BwA[ia], BuT], writes=[BPS[pa]])
                    for kc in range(KC):
                        kb.op("pe", MM(PS[pb][:, 0:TW], wA[ib][:, kc, jj * 128:(jj + 1) * 128], uT[:, kc, :],
                                       kc == 0, kc == KC - 1), reads=[BwA[ib], BuT], writes=[BPS[pb]])
                    si = sa_ring.next()
                    kb.op("act", ACTF(sa[si][:, :], PS[pa][:, 0:TW], AF.Silu), reads=[BPS[pa]], writes=[Bsa[si]])
                    kb.op("dve", TT(gT[:, j, :], sa[si][:, :], PS[pb][:, 0:TW], ALU.mult),
                          reads=[Bsa[si], BPS[pb]], writes=[BgT])
            pipe.add(load, compute)
        wo = wfull[wout].rearrange("(fc p) n -> p fc n", p=128)
        for g in range(4):
            bufs = {}

            def load(g=g, bufs=bufs):
                io = wO_ring.next()
                bufs["o"] = io
                kb.dma("sp", wO[io], wo[:, :, g * 256:(g + 1) * 256], reads=[Bwfull[wout]], writes=[BwO[io]])

            def compute(g=g, bufs=bufs):
                io = bufs["o"]
                for jj in range(2):
                    dc = g * 2 + jj
                    pi = main_ring.next()
                    for fc in range(FC):
                        kb.op("pe", MM(PS[pi][:, 0:TW], wO[io][:, fc, jj * 128:(jj + 1) * 128], gT[:, fc, :],
                                       fc == 0, fc == FC - 1), reads=[BwO[io], BgT], writes=[BPS[pi]])
                    kb.op("act", ACTF(yTf[:, dc * TW:(dc + 1) * TW], PS[pi][:, 0:TW], AF.Copy),
                          reads=[BPS[pi]], writes=Byt)
            pipe.add(load, compute)
        pipe.add(None, lambda: post_norm_add(t, lambda kc: gh[:, gpost * 8 + kc: gpost * 8 + kc + 1]))

    def proj_items(pipe, wname, col0, nchunks, consumer):
        wv = wview(wname)
        ngrp = (nchunks + 1) // 2
        for g in range(ngrp):
            bufs = {}
            ncg = min(2, nchunks - g * 2)

            def load(g=g, bufs=bufs, ncg=ncg):
                ia = wA_ring.next()
                bufs["a"] = ia
                kb.dma("sp", wA[ia][:, :, 0:ncg * 128], wv[:, :, col0 + g * 256: col0 + g * 256 + ncg * 128],
                       reads=[Bwfull[wname]], writes=[BwA[ia]])

            def compute(g=g, bufs=bufs, ncg=ncg):
                ia = bufs["a"]
                for jj in range(ncg):
                    ci = g * 2 + jj
                    pi = main_ring.next()
                    for kc in range(KC):
                        kb.op("pe", MM(PS[pi][:, 0:TW], wA[ia][:, kc, jj * 128:(jj + 1) * 128], uT[:, kc, :],
                                       kc == 0, kc == KC - 1), reads=[BwA[ia], BuT], writes=[BPS[pi]])
                    consumer(ci, pi)
            pipe.add(load, compute)

    pid_cache = {}

    def dynsrc(eng, fn):
        if "pid" not in pid_cache:
            pid_cache["pid"] = eng.partition_id()
        return fn(pid_cache["pid"])

    def dma_dyn(out, fn, reads, writes):
        idx = kb._next_dma_idx("sp")
        prev = kb.dma_val[idx]
        ex = [(("dma", idx), prev)] if prev else []
        waits = kb._deps("sp", reads, writes, ex)
        kb.dma_val[idx] = prev + 16
        tok = (("dma", idx), prev + 16)
        sem = kb.dma_sems[idx]

        def thunk(eng, waits=waits, sem=sem, out=out, fn=fn):
            for s, v in waits:
                eng.wait_ge(s, v)
            src = dynsrc(eng, fn)
            try:
                eng.dma_start(out=out, in_=src).then_inc(sem, 16)
            except Exception:
                print("DYN DMA FAIL", out, src)
                raise
        kb.prog["sp"].append(thunk)
        kb._commit(tok, reads, writes)
        return tok

    def fetch_tile(t):
        dma_dyn(L1all[:, :, :, t * TW:(t + 1) * TW],
                lambda pid, t=t: G1[t][:, :, bass.ds(pid, 1), :, :].rearrange("r k o d t -> r k (o d) t"),
                [BG1t[t]], [BL1])
        dma_dyn(LF[:, t * TW:(t + 1) * TW],
                lambda pid, t=t: GF[t][:, bass.ds(pid, 1), :].rearrange("r o t -> r (o t)"), [BGFt[t]], [BLF])

    pipeA = Pipe(2)
    for t in range(NTILE):
        ffn_items(pipeA, t, "f1in", "f1out", 0, 1)
        pipeA.add(None, (lambda t=t: make_u(t, 2)))

        def qkv_consumer(ci, pi, t=t):
            kind, hp = ci // 4, ci % 4
            si = st_ring.next()
            if kind == 0:
                kb.op("dve", TS(stage[si][:, :], PS[pi][:, 0:TW], 0.125, None, ALU.mult),
                      reads=[BPS[pi]], writes=[Bstage[si]])
            else:
                kb.op("act", ACTF(stage[si][:, :], PS[pi][:, 0:TW], AF.Copy), reads=[BPS[pi]],
                      writes=[Bstage[si]])
            dst = S1[t][kind, 2 * hp:2 * hp + 2, :, :].rearrange("h d t -> (h d) t")
            kb.dma("sp", dst, stage[si][:, :], reads=[Bstage[si]], writes=[BS1t[t]])
        proj_items(pipeA, "win", 0, 12, qkv_consumer)

        def f_load(t=t):
            kb.dma("sp", wf8[:, :, :], wview("win")[:, :, 1536:1544], reads=[Bwfull["win"]], writes=[Bwf8])

        def f_compute(t=t):
            pi = aux_ring.next()
            for kc in range(KC):
                kb.op("pe", MM(PS[pi][0:H, 0:TW], wf8[:, kc, :], uT[:, kc, :], kc == 0, kc == KC - 1),
                      reads=[Bwf8, BuT], writes=[BPS[pi]])
            kb.op("act", ACTF(lfT[:, :], PS[pi][0:H, 0:TW], AF.Exp, bias=nbf[:, 0:1], scale=-1.0),
                  reads=[BPS[pi], Bconst], writes=[BlfT])
            kb.op("act", ACTF(lfT[:, :], lfT[:, :], AF.Ln, bias=onet[0:H, 0:1], scale=1.0),
                  reads=[BlfT, Bconst], writes=[BlfT])
            kb.op("dve", TS(lfT[:, :], lfT[:, :], -1.0, None, ALU.mult), reads=[BlfT], writes=[BlfT])
            kb.dma("sp", SF[t], lfT[:, :], reads=[BlfT], writes=[BSFt[t]])
        pipeA.add(f_load, f_compute)

        def gather_tile(t=t):
            kb.collective([S1[t].rearrange("k h d t -> (k h d) t")], [G1[t].rearrange("r k h d t -> (r k h d) t")],
                          reads=[BS1t[t]], writes=[BG1t[t]])
            kb.collective([SF[t]], [GF[t].rearrange("r h t -> (r h) t")], reads=[BSFt[t]], writes=[BGFt[t]])
            if t >= 1:
                fetch_tile(t - 1)
        pipeA.add(None, gather_tile)
    if stop is not None and stop.startswith("A:"):
        pipeA.run(int(stop[2:]))
        return finish()
    pipeA.run()

    if stop == "A":
        return finish()
    fetch_tile(NTILE - 1)
    kb.barrier()

    s_ring = Ring([0, 1, 2, 3])
    pt_ring = Ring([0, 1, 2, 3])
    acc_ring = Ring([4, 5])
    o_ring = Ring([0, 1])

    kb.op("pool", MEMSET(Vaug[:, :, 64:128], 1.0), writes=[BVaug])
    for b in range(2):
        r0 = 4 * b
        kb.op("pool", MEMSET(Qaug[0:32, :], 0.0), writes=[BQaug])
        kb.op("pool", MEMSET(Kaug[0:32, :], 0.0), writes=[BKaug])
        kb.op("pool", MEMSET(Kaug[0:2, :], 1.0), writes=[BKaug])
        kb.dma("sp", Kaug[32:96, 0:NMETA], L1all[r0, 1, :, 0:NMETA], reads=[BL1], writes=[BKaug])
        for i in range(4):
            kb.dma("sp", Kaug[32:96, NMETA + i * TL: NMETA + (i + 1) * TL], L1all[r0 + i, 1, :, EX:NT],
                   reads=[BL1], writes=[BKaug])
            kb.dma("sp", Qaug[32:96, i * TL:(i + 1) * TL], L1all[r0 + i, 0, :, EX:NT], reads=[BL1], writes=[BQaug])
        kb.dma("sp", VTst[0:64, 0:NMETA], L1all[r0, 2, :, 0:NMETA], reads=[BL1], writes=[BVTst])
        pi = aux_ring.next()
        kb.op("pe", MM(PS[pi][0:NMETA, 0:64], VTst[0:64, 0:NMETA], Ibf[0:64, 0:64], True, True),
              reads=[BVTst, Bconst], writes=[BPS[pi]])
        kb.op("dve", CP(Vaug[0:NMETA, 0, 0:64], PS[pi][0:NMETA, 0:64]), reads=[BPS[pi]], writes=[BVaug])
        for i in range(4):
            kb.dma("sp", VTst[0:64, :], L1all[r0 + i, 2, :, EX:NT], reads=[BL1], writes=[BVTst])
            for g in range(2):
                pi = aux_ring.next()
                for q in range(8):
                    blk = g * 8 + q
                    kb.op("pe", MM(PS[pi][:, q * 64:(q + 1) * 64], VTst[0:64, blk * 128:(blk + 1) * 128],
                                   Ibf[0:64, 0:64], True, True), reads=[BVTst, Bconst], writes=[BPS[pi]])
                kb.op("dve", CP(Vaug[:, 1 + i * 16 + g * 8: 1 + i * 16 + g * 8 + 8, 0:64],
                                PS[pi][:, :].rearrange("p (q c) -> p q c", q=8)),
                      reads=[BPS[pi]], writes=[BVaug])
        kb.op("dve", MEMSET(lfR[:, :], 0.0), writes=[BlfR])
        kb.dma("sp", lfR[0:1, 0:NMETA], LF[r0:r0 + 1, 0:NMETA], reads=[BLF], writes=[BlfR])
        for i in range(4):
            kb.dma("sp", lfR[1 + 16 * i: 17 + 16 * i, :],
                   LF[r0 + i:r0 + i + 1, EX:NT].rearrange("o (a c) -> (o a) c", c=128), reads=[BLF], writes=[BlfR])
        pi = aux_ring.next()
        kb.op("pe", MM(PS[pi][:, 0:NBLK], lfR[:, :], I32[0:NBLK, 0:NBLK], True, True),
              reads=[BlfR, Bconst], writes=[BPS[pi]])
        kb.op("dve", CP(lfC[:, :], PS[pi][:, 0:NBLK]), reads=[BPS[pi]], writes=[BlfC])
        pi = aux_ring.next()
        kb.op("pe", MM(PS[pi][0:NBLK, 0:128], lfC[:, :], ONE32, True, True), reads=[BlfC, Bconst],
              writes=[BPS[pi]])
        kb.op("dve", CP(totT[:, :], PS[pi][0:NBLK, 0:128]), reads=[BPS[pi]], writes=[BtotT])
        pi = aux_ring.next()
        kb.op("pe", MM(PS[pi][:, 0:NBLK], U32, lfC[:, :], True, False), reads=[BlfC, Bconst], writes=[BPS[pi]])
        kb.op("pe", MM(PS[pi][:, 0:NBLK], totT[:, :], SU32[0:NBLK, 0:NBLK], False, True),
              reads=[BtotT, Bconst], writes=[BPS[pi]])
        kb.op("dve", TS(negF[:, :], PS[pi][:, 0:NBLK], -1.0, None, ALU.mult), reads=[BPS[pi]], writes=[BnegF])
        kb.op("dve", CP(Ff[:, :], PS[pi][:, 0:NBLK]), reads=[BPS[pi]], writes=[BFf])
        kb.op("dve", CP(Fp[:, :, 0], Ff[:, :]), reads=[BFf], writes=[BFp])
        kb.op("dve", TT(Fp[:, :, 1], Ff[:, :], Fp[:, :, 0], ALU.subtract), reads=[BFf, BFp], writes=[BFp])
        for g in range(16):
            pi = aux_ring.next()
            for q in range(4):
                rb = g * 4 + q
                kb.op("pe", MM(PS[pi][0:2, q * 128:(q + 1) * 128], Fp[:, rb + 1, :], Ibf, True, True),
                      reads=[BFp, Bconst], writes=[BPS[pi]])
            kb.op("dve", CP(Qaug[0:2, g * 512:(g + 1) * 512], PS[pi][0:2, :]), reads=[BPS[pi]], writes=[BQaug])

        steps = []
        for qc in range(16):
            blocks = [(-1, 0)] + [(rb, 0) for rb in range(4 * qc)] + [(4 * qc + r, r) for r in range(4)]
            for bi, (rb, r) in enumerate(blocks):
                steps.append((qc, bi, len(blocks), rb, r))
        LA = 2
        st = {}

        def s_part(i):
            qc, bi, nb, rb, r = steps[i]
            if bi == 0:
                st[("acc", qc)] = acc_ring.next()
            q0 = qc * 512
            if rb < 0:
                nk, kcol, vb = NMETA, 0, 0
            else:
                nk, kcol, vb = 128, NMETA + rb * 128, rb + 1
            diag = rb >= 4 * qc
            N = 512 - 128 * r
            si = s_ring.next()
            kb.op("pe", MM(PS[si][0:nk, 0:N], Kaug[0:96, kcol:kcol + nk], Qaug[0:96, q0 + r * 128: q0 + 512],
                           True, not diag), reads=[BKaug, BQaug], writes=[BPS[si]])
            if diag:
                kb.op("pe", MM(PS[si][:, 0:128], Ibf, TRIbf, False, True), reads=[Bconst], writes=[BPS[si]])
            pti = pt_ring.next()
            kb.op("act", ACTF(PT[pti][0:nk, 0:N], PS[si][0:nk, 0:N], AF.Exp, bias=negF[0:nk, vb:vb + 1]),
                  reads=[BPS[si], BnegF], writes=[BPT[pti]])
            st[i] = (pti, nk, vb, N)

        def pv_part(i):
            qc, bi, nb, rb, r = steps[i]
            pti, nk, vb, N = st.pop(i)
            ai = st[("acc", qc)]
            kb.op("pe", MM(PS[ai][:, r * 128:512], Vaug[0:nk, vb, :], PT[pti][0:nk, 0:N],
                           bi == 0, bi == nb - 1, inc=True), reads=[BVaug, BPT[pti]], writes=[BPS[ai]])
            if bi == nb - 1:
                oi = o_ring.next()
                kb.op("dve", CP(osb[oi][:, :], PS[ai][:, :]), reads=[BPS[ai]], writes=[Bosb[oi]])
                kb.dma("sp", den[oi][:, :], osb[oi][64:128, :], reads=[Bosb[oi]], writes=[Bden[oi]])
                kb.op("dve", RECIP(den[oi][:, :], den[oi][:, :]), reads=[Bden[oi]], writes=[Bden[oi]])
                kb.op("dve", TT(abf[oi][:, :], osb[oi][0:64, :], den[oi][:, :], ALU.mult),
                      reads=[Bosb[oi], Bden[oi]], writes=[Babf[oi]])
                tcid = 4 * b + qc // 4
                kb.dma("sp", S2[tcid, :, (qc % 4) * 512:(qc % 4 + 1) * 512], abf[oi][:, :], reads=[Babf[oi]],
                       writes=[BS2])

        nst = len(steps)
        for i in range(nst + LA):
            if i < nst:
                s_part(i)
            if i - LA >= 0:
                pv_part(i - LA)

    if stop == "B":
        return finish()
    kb.collective([S2.rearrange("c d t -> (c d) t")], [G2.rearrange("h c d t -> (h c d) t")],
                  reads=[BS2], writes=[BG2])
    kb.barrier()

    dma_dyn(L2, lambda pid: G2[:, bass.ds(pid, 1), :, :].rearrange("h o d t -> h (o d) t"), [BG2], [BL2])
    kb.op("pool", MEMSET(ccbuf[:, :, 0:2], 0.0), writes=[Bcc])
    pipeC = Pipe(2)
    yv = yTf[:, :].rearrange("p (k c) -> p k c", k=KC)
    for t in range(NTILE):
        c0 = t * TW
        pipeC.add(None, (lambda t=t: make_u(t, 2)))

        def attn_load(t=t):
            a0 = EX if t == 0 else 0
            tl0 = t * TW + a0 - EX
            if t == 0:
                kb.op("pool", MEMSET(attnT[:, :, 0:EX], 0.0), writes=[BattnT])
            for wi, wname in enumerate(("wab", "wcb")):
                kb.dma("sp", wBr[wi], wfull[wname].rearrange("(kc p) n -> p kc n", p=128),
                       reads=[Bwfull[wname]], writes=[BwO[wi]])
            for h in range(H):
                kb.dma("sp", attnT[(h % 2) * 64:(h % 2) * 64 + 64, h // 2, a0:TW], L2[h, :, tl0:tl0 + TW - a0],
                       reads=[BL2], writes=[BattnT])

        def conv_consumer(ci, pi, t=t):
            grp, ch = ci // 4, ci % 4
            if grp == 0:
                kb.op("act", ACTF(cbT[:, ch, :], PS[pi][:, 0:TW], AF.Copy), reads=[BPS[pi]], writes=[BcbT])
            elif grp == 1:
                kb.op("act", ACTF(ccbuf[:, ch, 2:2 + TW], PS[pi][:, 0:TW], AF.Copy), reads=[BPS[pi]],
                      writes=[Bcc])
            else:
                kb.op("dve", TT(ccbuf[:, ch, 2:2 + TW], ccbuf[:, ch, 2:2 + TW], PS[pi][:, 0:TW], ALU.mult),
                      reads=[BPS[pi], Bcc], writes=[Bcc])
                ti = tmp_ring.next()
                kb.op("dve", TS(tmp[ti][:, :], ccbuf[:, ch, 0:TW], cw[:, 0 * 4 + ch: 0 * 4 + ch + 1], None, ALU.mult),
                      reads=[Bcc, Bconst], writes=[Btmp[ti]])
                kb.op("dve", STT(tmp[ti][:, :], ccbuf[:, ch, 1:1 + TW], cw[:, 1 * 4 + ch: 1 * 4 + ch + 1],
                                 tmp[ti][:, :], ALU.mult, ALU.add), reads=[Bcc, Bconst, Btmp[ti]], writes=[Btmp[ti]])
                kb.op("dve", STT(tmp[ti][:, :], ccbuf[:, ch, 2:2 + TW], cw[:, 2 * 4 + ch: 2 * 4 + ch + 1],
                                 tmp[ti][:, :], ALU.mult, ALU.add), reads=[Bcc, Bconst, Btmp[ti]], writes=[Btmp[ti]])
                kb.op("dve", TT(convg[:, ch, :], tmp[ti][:, :], cbT[:, ch, :], ALU.mult),
                      reads=[Btmp[ti], BcbT], writes=[Bconvg])
                kb.op("dve", CP(ccbuf[:, ch, 0:2], ccbuf[:, ch, TW:TW + 2]), reads=[Bcc], writes=[Bcc])
        pipeC.add(attn_load, lambda: None)
        proj_items(pipeC, "win", 1544, 12, conv_consumer)

        def gate_consumer(ci, pi, t=t):
            which, dc = ci // 8, ci % 8
            src = attnT if which == 0 else convg
            Bsrc = BattnT if which == 0 else Bconvg
            gi = sig_ring.next()
            kb.op("act", ACTF(sig[gi][:, :], PS[pi][:, 0:TW], AF.Sigmoid), reads=[BPS[pi]], writes=[Bsig[gi]])
            p2 = main_ring.next()
            for kc in range(4):
                kb.op("pe", MM(PS[p2][:, 0:TW], wBr[which][:, kc, dc * 128:(dc + 1) * 128], src[:, kc, :],
                               kc == 0, kc == 3), reads=[BwO[which], Bsrc], writes=[BPS[p2]])
            if which == 0:
                kb.op("dve", TT(yv[:, dc, :], sig[gi][:, :], PS[p2][:, 0:TW], ALU.mult),
                      reads=[Bsig[gi], BPS[p2]], writes=Byt)
            else:
                ti = tmp_ring.next()
                kb.op("dve", TT(tmp[ti][:, :], sig[gi][:, :], PS[p2][:, 0:TW], ALU.mult),
                      reads=[Bsig[gi], BPS[p2]], writes=[Btmp[ti]])
                kb.op("dve", TT(gT[:, dc, :], tmp[ti][:, :], yv[:, dc, :], ALU.add),
                      reads=[Btmp[ti]] + Byt, writes=[BgT])
        proj_items(pipeC, "win", 3080, 16, gate_consumer)

        def wout_items(t=t):
            wv = wview("wout")
            for g in range(4):
                bufs = {}

                def load(g=g, bufs=bufs):
                    ia = wA_ring.next()
                    bufs["a"] = ia
                    kb.dma("sp", wA[ia], wv[:, :, g * 256:(g + 1) * 256], reads=[Bwfull["wout"]], writes=[BwA[ia]])

                def compute(g=g, bufs=bufs):
                    ia = bufs["a"]
                    for jj in range(2):
                        dc = g * 2 + jj
                        pi = main_ring.next()
                        for kc in range(KC):
                            kb.op("pe", MM(PS[pi][:, 0:TW], wA[ia][:, kc, jj * 128:(jj + 1) * 128], gT[:, kc, :],
                                           kc == 0, kc == KC - 1), reads=[BwA[ia], BgT], writes=[BPS[pi]])
                        kb.op("act", ACTF(yv[:, dc, :], PS[pi][:, 0:TW], AF.Copy), reads=[BPS[pi]], writes=Byt)
                pipeC.add(load, compute)
        wout_items()
        pipeC.add(None, (lambda t=t: post_norm_add(t, lambda kc: gn[:, 3 * 8 + kc: 3 * 8 + kc + 1])))
        ffn_items(pipeC, t, "f2in", "f2out", 4, 5)

        def out_store(t=t):
            c0 = t * TW
            a0 = EX if t == 0 else 0
            col = c0 + a0
            while col < c0 + TW:
                n = min(128, c0 + TW - col)
                xi = (col // 128) % 2
                xb, Bx = xs[xi], Byt[xi]
                for half in range(2):
                    pi = main_ring.next()
                    for q in range(4):
                        kc = half * 4 + q
                        kb.op("pe", MM(PS[pi][0:n, q * 128:(q + 1) * 128], hT[:, kc, col:col + n], I32,
                                       True, True), reads=[BhT[t], Bconst], writes=[BPS[pi]])
                    eng = cpe.next()
                    if eng == "dve":
                        kb.op("dve", CP(xb[0:n, half * 512:(half + 1) * 512], PS[pi][0:n, :]), reads=[BPS[pi]],
                              writes=[Bx])
                    else:
                        kb.op("act", ACTF(xb[0:n, half * 512:(half + 1) * 512], PS[pi][0:n, :], AF.Copy),
                              reads=[BPS[pi]], writes=[Bx])
                kb.dma("sp", y_out[col - EX: col - EX + n, :], xb[0:n, :], reads=[Bx])
                col += n
        pipeC.add(None, out_store)
    if stop is not None and stop.startswith("C:"):
        pipeC.run(int(stop[2:]))
        return finish()
    pipeC.run()

    toks = kb.all_tokens()
    waits = kb._deps("sp", (), (), toks)

    def fin(eng, waits=waits):
        for s, v in waits:
            eng.wait_ge(s, v)
    kb.prog["sp"].append(fin)
    kb.emit()
    return nc


PELT = "dve"
_NC_CACHE = {}
_STOP = None


def _consts():
    c = np.zeros((128, 640), np.float32)
    i = np.arange(128)
    c[:, 0:128] = np.eye(128, dtype=np.float32)
    c[:, 128:256] = (i[:, None] <= i[None, :]).astype(np.float32)
    c[:, 256:384] = (i[:, None] < i[None, :]).astype(np.float32)
    c[:, 384:512] = 1.0
    c[:, 512:640] = (i[:, None] > i[None, :]).astype(np.float32) * -30000.0
    return c


def kernel(x, meta_tokens, w_in, b_forget, conv_w, w_attn_branch, w_conv_branch, w_out,
           g_ffn1_pre, g_ffn1_post, w_ffn1_in, w_ffn1_out,
           g_mix_pre, g_mix_post, g_ffn2_pre, g_ffn2_post, w_ffn2_in, w_ffn2_out):
    f = lambda a: np.ascontiguousarray(np.asarray(a, dtype=np.float32))
    x = f(x)
    meta = f(meta_tokens)
    ws = {"f1in": f(w_ffn1_in)[0], "f1out": f(w_ffn1_out)[0], "win": f(w_in)[0], "wab": f(w_attn_branch)[0],
          "wcb": f(w_conv_branch)[0], "wout": f(w_out)[0], "f2in": f(w_ffn2_in)[0], "f2out": f(w_ffn2_out)[0]}
    gains = [f(g)[0] for g in (g_ffn1_pre, g_ffn1_post, g_mix_pre, g_mix_post, g_ffn2_pre, g_ffn2_post)]
    gn = np.zeros((128, 48), np.float32)
    for i, g in enumerate(gains):
        gn[:, i * 8:(i + 1) * 8] = g.reshape(8, 128).T
    cwm = np.zeros((128, 12), np.float32)
    cwv = f(conv_w)[0]
    for j in range(3):
        cwm[:, j * 4:(j + 1) * 4] = cwv[j].reshape(4, 128).T
    bfg = f(b_forget)[0].reshape(8, 1)
    cst = _consts()

    if "nc" not in _NC_CACHE:
        _NC_CACHE["nc"] = build_nc(_STOP)
    nc = _NC_CACHE["nc"]

    in_maps = []
    for c in range(NCORES):
        b, j = c // 4, c % 4
        xe = np.zeros((EX, D), np.float32)
        xe[0:NMETA] = meta
        xe[EX - 2:EX] = meta[NMETA - 2:NMETA] if j == 0 else x[b, j * TL - 2: j * TL]
        m = {"x": np.ascontiguousarray(x[b, j * TL:(j + 1) * TL]), "xe": xe, "gn": gn, "cw": cwm,
             "bfg": bfg, "cst": cst}
        for name, K, N in WSPEC:
            rows = K // NCORES
            m["w_" + name] = np.ascontiguousarray(ws[name][c * rows:(c + 1) * rows])
        in_maps.append(m)
    res = run_bass_kernel_spmd(nc, in_maps, core_ids=list(range(NCORES)))
    out = np.zeros((2, SEQ, D), np.float32)
    for c in range(NCORES):
        b, j = c // 4, c % 4
        out[b, j * TL:(j + 1) * TL] = res.results[c]["y"]
    return out
```
